# Optimizing a Trainium2 kernel written in Bass

```python
import math
import jax, jax.numpy as jnp
from jax import lax
import numpy as np

D_MODEL = 1024
BATCH = 2
SEQ = 8192
DEPTH = 1
DEC_BATCH = 32
DEC_SEQ = 16
PAST_LEN = 1024

CHUNK = 64
Q_BLOCK = 128

GLA_HEADS = 4
GLA_DK = (D_MODEL // 2) // GLA_HEADS
GLA_DV = D_MODEL // GLA_HEADS
GLA_RANK = 16
GLA_TAU = 16.0
GLA_BLOCK = 16

DIFF_HEAD_DIM = 64
DIFF_HEADS = D_MODEL // (2 * DIFF_HEAD_DIM)

D_FF = 2816
CONV_W = 3

EPS = 1e-5
DEEPNORM_ALPHA = (2.0 * DEPTH) ** 0.25
DEEPNORM_BETA = (8.0 * DEPTH) ** -0.25

GLA_QK = GLA_HEADS * GLA_DK
GLA_V = GLA_HEADS * GLA_DV
DIFF_QK = DIFF_HEADS * 2 * DIFF_HEAD_DIM
DIFF_V = DIFF_HEADS * 2 * DIFF_HEAD_DIM
IN_SPLITS = (GLA_QK, GLA_QK, GLA_V, GLA_V, GLA_RANK, DIFF_QK, DIFF_QK, DIFF_V, D_MODEL, D_MODEL)
IN_OFFSETS = tuple(int(o) for o in np.cumsum(IN_SPLITS)[:-1])
D_IN = int(sum(IN_SPLITS))

kernel_name = "streaming_gla_diffattn_convffn_deepnorm"


def _layer_norm(x, g, b):
    xf = x.astype(jnp.float32)
    mu = jnp.mean(xf, -1, keepdims=True)
    var = jnp.mean(jnp.square(xf - mu), -1, keepdims=True)
    return ((xf - mu) * lax.rsqrt(var + EPS) * g.astype(jnp.float32) + b.astype(jnp.float32)).astype(x.dtype)


def _rms_norm(x, g):
    xf = x.astype(jnp.float32)
    return xf * lax.rsqrt(jnp.mean(xf * xf, -1, keepdims=True) + EPS) * g.astype(jnp.float32)


def _gla_chunked(q, k, v, log_a, s0):
    b, t, h, _ = q.shape
    dv = v.shape[-1]
    n = t // GLA_BLOCK

    def blocks(z):
        return jnp.moveaxis(z.astype(jnp.float32).reshape(b, n, GLA_BLOCK, h, z.shape[-1]), 1, 0)

    causal = jnp.tril(jnp.ones((GLA_BLOCK, GLA_BLOCK), bool))

    def step(s, blk):
        qb, kb, vb, lb = blk
        cum = jnp.cumsum(lb, axis=1)
        q_dec = qb * jnp.exp(cum)
        k_inv = kb * jnp.exp(-cum)
        att = jnp.einsum('bthk,bshk->bhts', q_dec, k_inv)
        att = jnp.where(causal, att, 0.0)
        o = jnp.einsum('bhts,bshv->bthv', att, vb) + jnp.einsum('bthk,bhkv->bthv', q_dec, s)
        last = cum[:, -1]
        k_end = kb * jnp.exp(last[:, None] - cum)
        s = jnp.exp(last)[..., None] * s + jnp.einsum('bshk,bshv->bhkv', k_end, vb)
        return s, o

    s_fin, o = lax.scan(step, s0.astype(jnp.float32), (blocks(q), blocks(k), blocks(v), blocks(log_a)))
    o = jnp.moveaxis(o, 0, 1).reshape(b, t, h, dv)
    return o, s_fin


def _diff_attend(q, k, v, mask, lam):
    s = jnp.einsum('bqhme,bkhme->bhmqk', q, k).astype(jnp.float32) * (DIFF_HEAD_DIM ** -0.5)
    s = jnp.where(mask, s, -jnp.inf)
    p = jax.nn.softmax(s, axis=-1)
    w = p[:, :, 0] - lam * p[:, :, 1]
    return jnp.einsum('bhqk,bkhe->bqhe', w.astype(v.dtype), v)


def _chunk_mask(q_pos, key_pos):
    return key_pos[None, :] < (q_pos[:, None] // CHUNK + 1) * CHUNK


def _diff_attn_prompt(q, k, v, lam):
    b, t = q.shape[:2]
    nb = t // Q_BLOCK
    q_blocks = jnp.moveaxis(q.reshape(b, nb, Q_BLOCK, DIFF_HEADS, 2, DIFF_HEAD_DIM), 1, 0)
    key_pos = jnp.arange(t)

    def one(args):
        qb, i = args
        q_pos = i * Q_BLOCK + jnp.arange(Q_BLOCK)
        return _diff_attend(qb, k, v, _chunk_mask(q_pos, key_pos), lam)

    o = lax.map(one, (q_blocks, jnp.arange(nb)))
    return jnp.moveaxis(o, 0, 1).reshape(b, t, DIFF_HEADS, 2 * DIFF_HEAD_DIM)


def _layer(x, past_k, past_v, gla_s0, ffn_past, weights, layer_idx):
    (w_in, w_a2, b_a, gla_norm_g, lam_q1, lam_k1, lam_q2, lam_k2, diff_norm_g,
     w_out, ln1_g, ln1_b, w_up, conv_w, conv_b, w_down, ln2_g, ln2_b) = weights
    bsz, t, _ = x.shape
    is_prompt = past_k is None

    proj = x @ w_in
    gq, gk, gv, gg, g_low, dq, dk, dv, gate_a, gate_b = jnp.split(proj, IN_OFFSETS, axis=-1)

    log_a = (jax.nn.log_sigmoid((g_low @ w_a2 + b_a).astype(jnp.float32)) / GLA_TAU).reshape(bsz, t, GLA_HEADS, GLA_DK)
    q = gq.reshape(bsz, t, GLA_HEADS, GLA_DK) * (GLA_DK ** -0.5)
    k = gk.reshape(bsz, t, GLA_HEADS, GLA_DK)
    v = gv.reshape(bsz, t, GLA_HEADS, GLA_DV)
    pad = (-t) % GLA_BLOCK
    if pad:
        pw = ((0, 0), (0, pad), (0, 0), (0, 0))
        q, k, v, log_a = jnp.pad(q, pw), jnp.pad(k, pw), jnp.pad(v, pw), jnp.pad(log_a, pw)
    s_dtype = x.dtype if is_prompt else gla_s0.dtype
    s0 = jnp.zeros((bsz, GLA_HEADS, GLA_DK, GLA_DV), jnp.float32) if is_prompt else gla_s0
    o_g, s_new = _gla_chunked(q, k, v, log_a, s0)
    o_g = _rms_norm(o_g[:, :t], gla_norm_g) * jax.nn.silu(gg.reshape(bsz, t, GLA_HEADS, GLA_DV).astype(jnp.float32))
    o_gla = o_g.reshape(bsz, t, GLA_V).astype(x.dtype)

    qd = dq.reshape(bsz, t, DIFF_HEADS, 2, DIFF_HEAD_DIM)
    kd = dk.reshape(bsz, t, DIFF_HEADS, 2, DIFF_HEAD_DIM)
    vd = dv.reshape(bsz, t, DIFF_HEADS, 2 * DIFF_HEAD_DIM)
    lam_init = 0.8 - 0.6 * math.exp(-0.3 * layer_idx)
    lam = (jnp.exp(jnp.sum(lam_q1.astype(jnp.float32) * lam_k1.astype(jnp.float32)))
           - jnp.exp(jnp.sum(lam_q2.astype(jnp.float32) * lam_k2.astype(jnp.float32))) + lam_init)
    if is_prompt:
        o_d = _diff_attn_prompt(qd, kd, vd, lam)
    else:
        past_len = past_k.shape[1]
        k_all = jnp.concatenate([past_k, kd.astype(past_k.dtype)], axis=1)
        v_all = jnp.concatenate([past_v, vd.astype(past_v.dtype)], axis=1)
        mask = _chunk_mask(past_len + jnp.arange(t), jnp.arange(past_len + t))
        o_d = _diff_attend(qd, k_all, v_all, mask, lam)
    o_diff = (_rms_norm(o_d, diff_norm_g) * (1.0 - lam_init)).reshape(bsz, t, DIFF_V).astype(x.dtype)

    mixed = jax.nn.sigmoid(gate_a) * o_gla + jax.nn.sigmoid(gate_b) * o_diff
    x = _layer_norm(DEEPNORM_ALPHA * x + mixed @ w_out, ln1_g, ln1_b)

    up = x @ w_up
    a, gate = jnp.split(up, [D_FF], axis=-1)
    left = jnp.zeros((bsz, CONV_W - 1, D_FF), a.dtype) if is_prompt else ffn_past.astype(a.dtype)
    a_full = jnp.concatenate([left, a], axis=1)
    a_conv = conv_b
    for j in range(CONV_W):
        a_conv = a_conv + a_full[:, j:j + t] * conv_w[j]
    h = jax.nn.gelu(a_conv) * gate
    x = _layer_norm(DEEPNORM_ALPHA * x + h @ w_down, ln2_g, ln2_b)

    ffn_new = a_full[:, -(CONV_W - 1):]
    return x, kd, vd, s_new.astype(s_dtype), ffn_new


def setup_inputs(seed: int = 0) -> dict:
    key = jax.random.key(seed)
    ks = jax.random.split(key, 26)
    nrm = lambda i, shape: jax.random.normal(ks[i], shape, jnp.float32)
    return {
        "x_prompt": nrm(0, (BATCH, SEQ, D_MODEL)),
        "x_sample": nrm(1, (DEC_BATCH, DEC_SEQ, D_MODEL)),
        "cache_diff_k": nrm(2, (DEPTH, DEC_BATCH, PAST_LEN, DIFF_HEADS, 2, DIFF_HEAD_DIM)),
        "cache_diff_v": nrm(3, (DEPTH, DEC_BATCH, PAST_LEN, DIFF_HEADS, 2 * DIFF_HEAD_DIM)),
        "state_gla": 0.5 * nrm(4, (DEPTH, DEC_BATCH, GLA_HEADS, GLA_DK, GLA_DV)),
        "cache_ffn_conv": nrm(5, (DEPTH, DEC_BATCH, CONV_W - 1, D_FF)),
        "w_in": nrm(6, (DEPTH, D_MODEL, D_IN)) * D_MODEL ** -0.5,
        "w_a2": nrm(7, (DEPTH, GLA_RANK, GLA_QK)) * GLA_RANK ** -0.5,
        "b_a": 0.1 * nrm(8, (DEPTH, GLA_QK)),
        "gla_norm_g": 1.0 + 0.02 * nrm(9, (DEPTH, GLA_DV)),
        "lam_q1": 0.1 * nrm(10, (DEPTH, DIFF_HEAD_DIM)),
        "lam_k1": 0.1 * nrm(11, (DEPTH, DIFF_HEAD_DIM)),
        "lam_q2": 0.1 * nrm(12, (DEPTH, DIFF_HEAD_DIM)),
        "lam_k2": 0.1 * nrm(13, (DEPTH, DIFF_HEAD_DIM)),
        "diff_norm_g": 1.0 + 0.02 * nrm(14, (DEPTH, 2 * DIFF_HEAD_DIM)),
        "w_out": nrm(15, (DEPTH, D_MODEL, D_MODEL)) * (D_MODEL ** -0.5 * DEEPNORM_BETA),
        "ln1_g": 1.0 + 0.02 * nrm(16, (DEPTH, D_MODEL)),
        "ln1_b": 0.02 * nrm(17, (DEPTH, D_MODEL)),
        "w_up": nrm(18, (DEPTH, D_MODEL, 2 * D_FF)) * D_MODEL ** -0.5,
        "conv_w": nrm(19, (DEPTH, CONV_W, D_FF)) * CONV_W ** -0.5,
        "conv_b": 0.02 * nrm(20, (DEPTH, D_FF)),
        "w_down": nrm(21, (DEPTH, D_FF, D_MODEL)) * (D_FF ** -0.5 * DEEPNORM_BETA),
        "ln2_g": 1.0 + 0.02 * nrm(22, (DEPTH, D_MODEL)),
        "ln2_b": 0.02 * nrm(23, (DEPTH, D_MODEL)),
    }


def reference(x_prompt, x_sample, cache_diff_k, cache_diff_v, state_gla, cache_ffn_conv,
              w_in, w_a2, b_a, gla_norm_g, lam_q1, lam_k1, lam_q2, lam_k2, diff_norm_g,
              w_out, ln1_g, ln1_b, w_up, conv_w, conv_b, w_down, ln2_g, ln2_b):
    yp, ys = x_prompt, x_sample
    pk, pv, ps, pc, sk, sv, ss, sc = [], [], [], [], [], [], [], []
    for l in range(DEPTH):
        weights = (w_in[l], w_a2[l], b_a[l], gla_norm_g[l], lam_q1[l], lam_k1[l], lam_q2[l], lam_k2[l],
                   diff_norm_g[l], w_out[l], ln1_g[l], ln1_b[l], w_up[l], conv_w[l], conv_b[l],
                   w_down[l], ln2_g[l], ln2_b[l])
        yp, k_new, v_new, s_new, c_new = _layer(yp, None, None, None, None, weights, l)
        pk.append(k_new)
        pv.append(v_new)
        ps.append(s_new)
        pc.append(c_new)
        ys, k_new, v_new, s_new, c_new = _layer(ys, cache_diff_k[l], cache_diff_v[l], state_gla[l],
                                                cache_ffn_conv[l], weights, l)
        sk.append(k_new)
        sv.append(v_new)
        ss.append(s_new)
        sc.append(c_new)
    return (yp, ys, jnp.stack(pk), jnp.stack(pv), jnp.stack(ps), jnp.stack(pc),
            jnp.stack(sk), jnp.stack(sv), jnp.stack(ss), jnp.stack(sc))
```

```python
import math
import os
from contextlib import ExitStack

import numpy as np

import concourse.bass as bass
import concourse.mybir as mybir
from concourse.bass_utils import run_bass_kernel_spmd

F32 = mybir.dt.float32
BF16 = mybir.dt.bfloat16
AF = mybir.ActivationFunctionType
ALU = mybir.AluOpType

D = 1024
KC = 8
DFF = 2816
NCH = 22
EPS = 1e-5
ALPHA = 2.0 ** 0.25
LAM_INIT = 0.8 - 0.6 * math.exp(-0.3 * 0)
NCORES = 8
WIN = 2064

ENGS = ['pe', 'act', 'dve', 'pool', 'sp']


class Op:
    __slots__ = ('eng', 'fn', 'deps', 'lane', 'inc', 'sig', 'semkey', 'val')

    def __init__(self, eng, fn, deps, lane=None, inc=1):
        self.eng = eng
        self.fn = fn
        self.deps = deps
        self.lane = lane
        self.inc = inc
        self.sig = lane is not None
        self.semkey = None
        self.val = None


class Prog:
    ROLL = int(os.environ.get("KROLL", "8000"))

    def __init__(self, nc, es):
        self.nc = nc
        self.es = es
        self.q = {e: [] for e in ENGS}
        self.last_w = {}
        self.readers = {}
        self.lane_last = {}
        self.lane_eng = {}
        self.pool_ctr = {}

    def _auto(self, r, w):
        deps = []
        for k in r:
            lw = self.last_w.get(k)
            if lw is not None:
                deps.append(lw)
        for k in w:
            lw = self.last_w.get(k)
            if lw is not None:
                deps.append(lw)
            deps.extend(self.readers.get(k, ()))
        return deps

    def _reg(self, o, r, w):
        for k in r:
            self.readers.setdefault(k, []).append(o)
        for k in w:
            self.last_w[k] = o
            self.readers[k] = []

    @staticmethod
    def _excl(r, w):
        pr = [k for k in r if isinstance(k, tuple) and k[0] == 'ps']
        if pr:
            r = [k for k in r if not (isinstance(k, tuple) and k[0] == 'ps')]
            w = list(w) + [k for k in pr if k not in w]
        return r, w

    def op(self, eng, fn, r=(), w=(), deps=()):
        r, w = self._excl(r, w)
        d = [x for x in deps if x is not None] + self._auto(r, w)
        o = Op(eng, fn, d)
        self.q[eng].append(o)
        self._reg(o, r, w)
        return o

    def dma(self, eng, fn, pool, r=(), w=(), deps=(), nlanes=4, inc=16):
        i = self.pool_ctr.get(pool, 0)
        self.pool_ctr[pool] = i + 1
        lane = (pool, i % nlanes)
        assert self.lane_eng.setdefault(lane, eng) == eng, lane
        r, w = self._excl(r, w)
        d = [x for x in deps if x is not None] + self._auto(r, w)
        prev = self.lane_last.get(lane)
        if prev is not None:
            d.append(prev)
        o = Op(eng, fn, d, lane=lane, inc=inc)
        self.lane_last[lane] = o
        self.q[eng].append(o)
        self._reg(o, r, w)
        return o

    def barrier(self):
        lasts = [self.q[e][-1] for e in ENGS if self.q[e]]
        lasts += list(self.lane_last.values())
        res = []
        for e in ENGS:
            o = Op(e, None, list(lasts))
            self.q[e].append(o)
            res.append(o)
        self.last_w = {}
        self.readers = {}
        return res

    def emit(self):
        nc = self.nc
        for e in ENGS:
            for o in self.q[e]:
                for d in o.deps:
                    if d.eng == 'pe' and o.eng == 'pe' and d.lane is None:
                        continue
                    d.sig = True
        keys = []
        lane_cnt = {}
        for e in ENGS:
            cnt = 0
            idx = 0
            for o in self.q[e]:
                if o.lane is not None:
                    c = lane_cnt.get(o.lane, 0) + o.inc
                    lane_cnt[o.lane] = c
                    o.semkey = ('lane',) + tuple(o.lane)
                    o.val = c
                elif o.sig and o.fn is not None:
                    cnt += 1
                    if cnt > self.ROLL:
                        idx += 1
                        cnt = 1
                    o.semkey = ('done', e, idx)
                    o.val = cnt
                if o.semkey is not None and o.semkey not in keys:
                    keys.append(o.semkey)
        sems = {}
        for i, k in enumerate(keys):
            sems[k] = self.es.enter_context(nc.semaphore("s%d" % i))
        self.n_sems = len(keys)
        self.n_ops = {e: len(self.q[e]) for e in ENGS}
        block = self.es.enter_context(nc.Block())

        def runner(eng):
            def run(h):
                waited = {}
                for o in self.q[eng]:
                    for d in o.deps:
                        if d.fn is None:
                            continue
                        if d.eng == 'pe' and eng == 'pe' and d.lane is None:
                            continue
                        if waited.get(d.semkey, 0) < d.val:
                            h.wait_ge(sems[d.semkey], d.val)
                            waited[d.semkey] = d.val
                    if o.fn is None:
                        continue
                    ins = o.fn(h)
                    if o.sig:
                        ins.then_inc(sems[o.semkey], o.inc)
                if eng == 'sp':
                    for ln, c in lane_cnt.items():
                        k = ('lane',) + tuple(ln)
                        if waited.get(k, 0) < c:
                            h.wait_ge(sems[k], c)
            return run

        block.tensor(runner('pe'))
        block.scalar(runner('act'))
        block.vector(runner('dve'))
        block.gpsimd(runner('pool'))
        block.sync(runner('sp'))


def MM(out, lhsT, rhs, start=True, stop=True, skip=False):
    if skip:
        return lambda e: e.matmul(out, lhsT=lhsT, rhs=rhs, start=start, stop=stop, skip_group_check=True)
    return lambda e: e.matmul(out, lhsT=lhsT, rhs=rhs, start=start, stop=stop)


def TR(out, in_, ident):
    return lambda e: e.transpose(out, in_, ident)


def ACT(out, in_, func, bias=None, scale=None, accum_out=None):
    kw = {}
    if bias is not None:
        kw['bias'] = bias
    if scale is not None:
        kw['scale'] = scale
    if accum_out is not None:
        kw['accum_out'] = accum_out
    return lambda e: e.activation(out=out, in_=in_, func=func, **kw)


def TT(out, in0, in1, op):
    return lambda e: e.tensor_tensor(out=out, in0=in0, in1=in1, op=op)


def TS(out, in0, s1, s2=None, op0=ALU.mult, op1=None):
    if op1 is None:
        return lambda e: e.tensor_scalar(out=out, in0=in0, scalar1=s1, scalar2=None, op0=op0)
    return lambda e: e.tensor_scalar(out=out, in0=in0, scalar1=s1, scalar2=s2, op0=op0, op1=op1)


def STT(out, in0, scalar, in1, op0, op1):
    return lambda e: e.scalar_tensor_tensor(out=out, in0=in0, scalar=scalar, in1=in1, op0=op0, op1=op1)


def CP(eng, out, in_):
    if eng == 'act':
        return lambda e: e.copy(out=out, in_=in_)
    return lambda e: e.tensor_copy(out=out, in_=in_)


def MS(ap, val):
    return lambda e: e.memset(ap, val)


def RC(out, in_):
    return lambda e: e.reciprocal(out=out, in_=in_)


def DMA(out, in_):
    return lambda e: e.dma_start(out=out, in_=in_)


class Arena:
    def __init__(self, base_ap, words):
        self.base = base_ap
        self.words = words
        self.off = 0

    def alloc(self, shape, dt):
        n = 1
        for s in shape:
            n *= s
        esz = 4 if dt == F32 else 2
        w = (n * esz + 3) // 4
        w = (w + 7) // 8 * 8
        assert self.off + w <= self.words, ("arena overflow", self.off, w, self.words)
        ap = self.base[:, self.off:self.off + w]
        self.off += w
        if dt != F32:
            ap = ap.bitcast(dt)
        ap = ap[:, 0:n]
        if len(shape) == 2:
            ap = ap.rearrange("p (a b) -> p a b", b=shape[1])
        elif len(shape) == 3:
            ap = ap.rearrange("p (a b c) -> p a b c", b=shape[1], c=shape[2])
        elif len(shape) == 4:
            ap = ap.rearrange("p (a b c d) -> p a b c d", b=shape[1], c=shape[2], d=shape[3])
        return ap


def build(TP=8192, debug=False, stage=7):
    NSUP = TP // 256
    NT1 = TP + 256
    CK = min(1024, TQ0 := TP // 4)
    SPC = CK // 256
    NCK = TP // CK
    CPQ = (TP // 4) // CK
    TQ = TP // 4
    N3 = 2 + TQ + 64
    NKT = TP // 128

    nc = bass.Bass("TRN2", target_bir_lowering=False)

    def din(name, shape, dt=F32):
        return nc.dram_tensor(name, list(shape), dt, kind="ExternalInput").ap()

    def dout(name, shape, dt=F32):
        return nc.dram_tensor(name, list(shape), dt, kind="ExternalOutput").ap()

    xT1 = din("xT1", [D, NT1])
    w_in1 = din("w_in1", [D, WIN])
    w_a2g = din("w_a2g", [16, 128])
    b_ag = din("b_ag", [128, 1])
    gng_bc_d = din("gng_bc", [128, 256])
    dng_bc_d = din("dng_bc", [128, 256])
    lamv_d = din("lamv", [128, 4, 64])
    ckT = din("ckT", [16, 2, 128, 1024])
    cv = din("cv", [16, 2, 128, 8, 128])
    s0_d = din("s0", [16, 128, 256])
    ident_d = din("ident", [128, 128])
    cm_d = din("cm", [128, 128])
    smk_d = din("smk", [128, 128])
    rmask_d = din("rmask", [128, 2, 256])
    bmask_d = din("bmask", [128, 8, 2, 16])
    x3 = din("x3", [N3, D])
    w_out_d = din("w_out", [D, D])
    w_up_d = din("w_up", [D, 2 * DFF])
    w_down_d = din("w_down", [DFF, D])
    lnbc_d = din("lnbc", [128, 4, D])
    convp_d = din("convp", [128, NCH, 4])
    cfc_d = din("cfc", [128, NCH, 4, 2])
    flag_d = din("flag", [128, 1])
    o_kv = dout("o_kv", [NT1, 512])
    o_S = dout("o_S", [17, 128, 256])
    y3 = dout("y3", [TQ + 64, D])
    o_fc = dout("o_fc", [128, NCH * 5 * 2])
    if debug:
        o_mixT = dout("o_mixT", [NCK + 1, 256, CK], BF16)
    mixedT = nc.dram_tensor("mixedT", [NCK + 1, 256, CK], BF16)
    gath = nc.dram_tensor("gath", [NCK + 2, 4 * 256, CK], BF16)
    wdn_bf = nc.dram_tensor("wdn_bf", [DFF, D], BF16)
    wup_bf = nc.dram_tensor("wup_bf", [D, 2 * DFF], BF16)
    wout_bf = nc.dram_tensor("wout_bf", [D, D], BF16)

    with ExitStack() as es:
        AW = 52800
        arena_t = es.enter_context(nc.sbuf_tensor("arena", [128, AW], F32))
        ps = es.enter_context(nc.psum_tensor("ps", [128, 8, 512], F32))
        ar = Arena(arena_t, AW)
        P = Prog(nc, es)

        ident = ar.alloc([128], BF16)
        zero2 = ar.alloc([8, 2], BF16)
        persist_off = ar.off

        w_in_sb = ar.alloc([KC, WIN], BF16)
        kdT = ar.alloc([2, TP], BF16)
        V_sb = ar.alloc([NKT, 2, 130], BF16)
        xb = [ar.alloc([KC, 256], BF16) for _ in range(2)]
        xs32 = ar.alloc([KC, 256], F32)
        A_bf = ar.alloc([2, 512], BF16)
        B_f = [ar.alloc([2, 512], F32) for _ in range(2)]
        dk_bf = ar.alloc([2, 256], BF16)
        v_bf = ar.alloc([2, 256], BF16)
        gg_f = ar.alloc([2, 256], F32)
        egg = ar.alloc([2, 256], F32)
        sigD2 = [ar.alloc([2, 512], F32) for _ in range(2)]
        glow_bf = ar.alloc([2, 128], BF16)
        qkT = ar.alloc([2, 256], BF16)
        qdT = [ar.alloc([2, 256], BF16) for _ in range(2)]
        glowT = ar.alloc([256], BF16)
        kdTs = ar.alloc([2, 256], BF16)
        Vs_sb = ar.alloc([2, 2, 130], BF16)
        w_a2_sb = ar.alloc([128], BF16)
        nb_a = ar.alloc([1], F32)
        Lb = ar.alloc([256], F32)
        cumL = ar.alloc([256], F32)
        eq = ar.alloc([256], F32)
        ek = ar.alloc([256], F32)
        eend = ar.alloc([256], F32)
        q_decT = ar.alloc([256], BF16)
        k_invT = ar.alloc([256], BF16)
        k_endT = ar.alloc([256], BF16)
        attT_bf = ar.alloc([128], BF16)
        k_end_bf = ar.alloc([128], BF16)
        S_f = ar.alloc([256], F32)
        S_bf = ar.alloc([256], BF16)
        S0_f = [ar.alloc([256], F32) for _ in range(2)]
        S0_bf = [ar.alloc([256], BF16) for _ in range(2)]
        Sn_f = [ar.alloc([256], F32) for _ in range(2)]
        qdm = ar.alloc([8, 128], BF16)
        kem = ar.alloc([8, 128], BF16)
        kend_m = ar.alloc([8, 128], BF16)
        mixa2 = [ar.alloc([2, 256], F32) for _ in range(2)]
        t1 = ar.alloc([256], F32)
        t2 = ar.alloc([256], F32)
        t3 = ar.alloc([256], F32)
        st4 = ar.alloc([16], F32)
        PT = [ar.alloc([2, 256], BF16) for _ in range(4)]
        O_sb = ar.alloc([4, 129], F32)
        rden = ar.alloc([8], F32)
        od = ar.alloc([2, 128], F32)
        od2 = ar.alloc([2, 128], F32)
        ssq = ar.alloc([8], F32)
        mix_f = ar.alloc([2, 256], F32)
        mix_bf = ar.alloc([2, 256], BF16)
        mixT = [ar.alloc([2, 256], BF16) for _ in range(2)]
        cm = ar.alloc([128], F32)
        smk = ar.alloc([128], F32)
        rmask = ar.alloc([2, 256], F32)
        gng_bc = ar.alloc([256], F32)
        dng_bc = ar.alloc([256], F32)
        bmask = ar.alloc([8, 2, 16], F32)
        lamv = ar.alloc([4, 64], F32)
        lamt = ar.alloc([2, 64], F32)
        lams = ar.alloc([8], F32)
        nlam = ar.alloc([1], F32)
        p1_end = ar.off
        KcT = [kdT[:, 0, 0:1024], kdT[:, 0, 1024:2048]] if TP >= 8192 else None
        if KcT is None:
            KcT = [ar.alloc([1024], BF16) for _ in range(2)]
            Vc = [ar.alloc([8, 130], BF16) for _ in range(2)]
            PTs = ar.alloc([2, 9, 128], BF16)
            Kst = [ar.alloc([1024], F32) for _ in range(2)]
            Vst = [ar.alloc([8, 128], F32) for _ in range(2)]
        else:
            st32 = kdT[:, 1, :].bitcast(F32)
            Kst = [st32[:, 0:1024], st32[:, 1024:2048]]
            Vst = [st32[:, 2048:3072].rearrange("p (a b) -> p a b", b=128),
                   st32[:, 3072:4096].rearrange("p (a b) -> p a b", b=128)]
            Vc = [kdT[:, 0, 2048:2048 + 1040].rearrange("p (a b) -> p a b", b=130),
                  kdT[:, 0, 3200:3200 + 1040].rearrange("p (a b) -> p a b", b=130)]
            PTs = kdT[:, 0, 4352:4352 + 2304].rearrange("p (a b c) -> p a b c", b=9, c=128)
        p1_total = ar.off

        def pSbuf(x):
            return ps[:, 2 * x:2 * x + 2, :]

        def pSk(x):
            return [('ps', 2 * x), ('ps', 2 * x + 1)]

        def pOacc(k):
            return ps[:, 4 + k // 3, (k % 3) * 129:(k % 3) * 129 + 129]

        def pOk(k):
            return ('ps', 4 + k // 3)
        M0 = ps[:, 6, :]
        M1 = ps[:, 7, :]
        M0b = ps[:, 6, :].bitcast(BF16)
        M1b = ps[:, 7, :].bitcast(BF16)
        K6 = ('ps', 6)
        K7 = ('ps', 7)

        for kc in range(KC):
            P.dma('pool', DMA(w_in_sb[:, kc, :], w_in1[kc * 128:(kc + 1) * 128, :]), 'wld', w=[('w_in', kc)])
        xT1v = xT1.rearrange("(kc p) t -> p kc t", p=128)

        def load_x(T):
            slot = T % 2
            P.dma('sp', DMA(xs32[:, :, :], xT1v[:, :, T * 256:(T + 1) * 256]), 'xld', w=['xs32'], nlanes=2)
            P.op('act', CP('act', xb[slot][:, 0:4, :], xs32[:, 0:4, :]), r=['xs32'], w=[('xb', slot)])
            P.op('dve', CP('dve', xb[slot][:, 4:8, :], xs32[:, 4:8, :]), r=['xs32', ('xb', slot)], w=[('xb', slot)])
        load_x(0)
        load_x(1)
        P.op('pool', MS(w_a2_sb[:, :], 0.0), w=['w_a2'])
        P.op('pool', MS(glow_bf[:, :, :], 0.0), w=['glow0'])
        P.dma('pool', DMA(w_a2_sb[0:16, :], w_a2g[:, :]), 'wld', r=['w_a2'], w=['w_a2'])
        P.dma('pool', DMA(ident[:, :], ident_d[:, :]), 'wld', w=['ident'])
        P.dma('sp', DMA(nb_a[:, :], b_ag[:, :]), 'cld', w=['nb_a'])
        P.dma('sp', DMA(cm[:, :], cm_d[:, :]), 'cld', w=['cm'])
        P.dma('sp', DMA(smk[:, :], smk_d[:, :]), 'cld', w=['smk'])
        P.dma('sp', DMA(rmask[:, :, :], rmask_d[:, :, :]), 'cld', w=['rmask'])
        P.dma('sp', DMA(gng_bc[:, :], gng_bc_d[:, :]), 'cld', w=['gng'])
        P.dma('sp', DMA(dng_bc[:, :], dng_bc_d[:, :]), 'cld', w=['dng'])
        P.dma('sp', DMA(bmask[:, :, :, :], bmask_d[:, :, :, :]), 'cld', w=['bmask'])
        P.dma('sp', DMA(lamv[:, :, :], lamv_d[:, :, :]), 'cld', w=['lamv'])
        bulk = []
        for c in range(0, NCH, 2):
            bulk.append((DMA(wdn_bf[c * 128:(c + 2) * 128, :], w_down_d[c * 128:(c + 2) * 128, :]), [('wdn', c), ('wdn', c + 1)]))
        for kc in range(KC):
            bulk.append((DMA(wup_bf[kc * 128:(kc + 1) * 128, :], w_up_d[kc * 128:(kc + 1) * 128, :]), [('wupd', kc)]))
        for kc in range(0, KC, 2):
            bulk.append((DMA(wout_bf[kc * 128:(kc + 2) * 128, :], w_out_d[kc * 128:(kc + 2) * 128, :]), [('woutd', kc)]))

        def issue_bulk(n):
            for _ in range(n):
                if bulk:
                    fn, wk = bulk.pop(0)
                    P.dma('pool', fn, 'wdc', w=wk)
        P.op('pool', TS(nb_a[:, :], nb_a[:, :], -1.0), r=['nb_a'], w=['nb_a'])
        P.op('pool', TS(dng_bc[:, :], dng_bc[:, :], 1.0 - LAM_INIT), r=['dng'], w=['dng'])
        P.op('dve', TT(lamt[:, 0, :], lamv[:, 0, :], lamv[:, 1, :], ALU.mult), r=['lamv'], w=['lamt'])
        P.op('dve', TT(lamt[:, 1, :], lamv[:, 2, :], lamv[:, 3, :], ALU.mult), r=['lamv', 'lamt'], w=['lamt'])
        P.op('dve', lambda e: e.reduce_sum(out=lams[:, 0:2], in_=lamt[:, :, :], axis=mybir.AxisListType.X),
             r=['lamt'], w=['lams'])
        P.op('act', ACT(lams[:, 2:4], lams[:, 0:2], AF.Exp), r=['lams'], w=['lams'])
        P.op('dve', TT(lams[:, 4:5], lams[:, 3:4], lams[:, 2:3], ALU.subtract), r=['lams'], w=['lams'])
        P.op('dve', TS(nlam[:, :], lams[:, 4:5], -LAM_INIT, None, ALU.add), r=['lams'], w=['nlam'])
        P.op('pool', MS(zero2[:, :, :], 0.0), w=['zero2'])
        P.op('pool', MS(V_sb[:, :, :, 128:130], 1.0), w=['Vones'])
        P.op('pool', MS(Vs_sb[:, :, :, 128:130], 1.0), w=['Vsones'])
        P.op('pool', MS(qdm[:, :, :], 0.0), w=['qdm'])
        P.op('pool', MS(kem[:, :, :], 0.0), w=['kem'])
        P.op('pool', MS(S_f[:, :], 0.0), w=['S_f'])
        P.op('pool', MS(S_bf[:, :], 0.0), w=['S_bf'])
        mz = mixedT.ap()
        gv = gath.ap()
        P.dma('sp', DMA(gv[0, :, CK - 2:CK].rearrange("(c p) t -> p c t", p=128), zero2[:, :, :]), 'cld', r=['zero2'],
              w=['gath_pad'])

        GRP = [(0, 512), (512, 512), (1024, 512), (1536, 512), (2048, 16)]
        live7 = [0]

        def prep(T):
            slot = T % 2
            sample = (T == NSUP)
            tok0 = T * 256
            bslot = T % 2
            sigD = sigD2[slot]
            mixa = mixa2[slot]
            gi = 0
            for sub in range(2):
                for g, (c0, cw) in enumerate(GRP):
                    bank = M0 if gi % 2 == 0 else M1
                    bkey = K6 if gi % 2 == 0 else K7
                    gi += 1
                    for kc in range(KC):
                        P.op('pe', MM(bank[:, 0:cw], xb[slot][:, kc, sub * 128:(sub + 1) * 128],
                                      w_in_sb[:, kc, c0:c0 + cw], start=(kc == 0), stop=(kc == KC - 1)),
                             r=[('xb', slot), ('w_in', kc)], w=[bkey])
                    if bkey == K7:
                        live7[0] += 1
                    yield
                    if bkey == K7:
                        live7[0] -= 1
                    if g == 0:
                        P.op('dve', CP('dve', A_bf[:, sub, :], bank[:, 0:512]), r=[bkey], w=[('A_bf', sub)])
                    elif g == 1:
                        P.op('dve', CP('dve', B_f[bslot][:, sub, :], bank[:, 0:512]), r=[bkey], w=[('B_f', bslot, sub)])
                        P.op('dve', CP('dve', dk_bf[:, sub, :], bank[:, 0:256]), r=[bkey], w=[('dk_bf', sub)])
                        if not sample:
                            P.op('dve', CP('dve', V_sb[:, 2 * T + sub, :, 0:128],
                                           bank[:, 256:512].rearrange("p (h e) -> p h e", e=128)),
                                 r=[bkey, 'Vones'], w=[('V', 2 * T + sub)])
                        else:
                            P.op('dve', CP('dve', Vs_sb[:, sub, :, 0:128],
                                           bank[:, 256:512].rearrange("p (h e) -> p h e", e=128)),
                                 r=[bkey, 'Vsones'], w=[('Vs', sub)])
                    elif g == 2:
                        P.op('dve', CP('dve', v_bf[:, sub, :], bank[:, 0:256]), r=[bkey], w=[('v_bf', sub)])
                        P.op('dve', CP('dve', gg_f[:, sub, :], bank[:, 256:512]), r=[bkey], w=[('gg_f', sub)])
                        P.op('act', ACT(egg[:, sub, :], bank[:, 256:512], AF.Exp, scale=-1.0), r=[bkey], w=[('egg', sub)])
                        P.op('act', ACT(egg[:, sub, :], egg[:, sub, :], AF.Ln, bias=1.0), r=[('egg', sub)], w=[('egg', sub)])
                        P.op('act', ACT(egg[:, sub, :], egg[:, sub, :], AF.Exp, scale=-1.0), r=[('egg', sub)], w=[('egg', sub)])
                    elif g == 3:
                        P.op('act', ACT(sigD[:, sub, :], bank[:, 0:512], AF.Exp, scale=-1.0), r=[bkey], w=[('sigD', slot, sub)])
                        P.op('act', ACT(sigD[:, sub, :], sigD[:, sub, :], AF.Ln, bias=1.0), r=[('sigD', slot, sub)],
                             w=[('sigD', slot, sub)])
                        P.op('act', ACT(sigD[:, sub, :], sigD[:, sub, :], AF.Exp, scale=-1.0), r=[('sigD', slot, sub)],
                             w=[('sigD', slot, sub)])
                    else:
                        P.op('dve', CP('dve', glow_bf[:, sub, 0:16], bank[:, 0:16]), r=[bkey, 'glow0'], w=[('glow_bf', sub)])
                yield
                for i in range(4):
                    P.op('pe', TR(M1b[:, i * 128:(i + 1) * 128], A_bf[:, sub, i * 128:(i + 1) * 128], ident[:, :]),
                         r=[('A_bf', sub), 'ident'], w=[K7])
                for i in range(2):
                    P.op('pe', TR(M1b[:, 512 + i * 128:512 + (i + 1) * 128], dk_bf[:, sub, i * 128:(i + 1) * 128],
                                  ident[:, :]), r=[('dk_bf', sub), 'ident'], w=[K7])
                P.op('pe', TR(M1b[:, 768:896], glow_bf[:, sub, :], ident[:, :]), r=[('glow_bf', sub), 'ident'], w=[K7])
                live7[0] += 1
                yield
                live7[0] -= 1
                scol = slice(sub * 128, (sub + 1) * 128)
                P.op('dve', CP('dve', qkT[:, :, scol], M1b[:, 0:256].rearrange("p (a t) -> p a t", t=128)),
                     r=[K7], w=[('qkT', sub)])
                P.op('dve', CP('dve', qdT[slot][:, :, scol], M1b[:, 256:512].rearrange("p (a t) -> p a t", t=128)),
                     r=[K7], w=[('qdT', slot)])
                if not sample:
                    P.op('dve', CP('dve', kdT[:, :, tok0 + sub * 128:tok0 + (sub + 1) * 128],
                                   M1b[:, 512:768].rearrange("p (a t) -> p a t", t=128)),
                         r=[K7], w=[('kdT', 2 * T + sub)])
                else:
                    P.op('dve', CP('dve', kdTs[:, :, scol], M1b[:, 512:768].rearrange("p (a t) -> p a t", t=128)),
                         r=[K7], w=[('kdTs', sub)])
                P.op('dve', CP('dve', glowT[:, scol], M1b[:, 768:896]), r=[K7], w=[('glowT', sub)])
                yield
            P.dma('sp', DMA(o_kv[tok0:tok0 + 256, :].rearrange("(s p) c -> p s c", p=128), B_f[bslot][:, :, :]), 'okv',
                  r=[('B_f', bslot, 0), ('B_f', bslot, 1)])
            P.op('pe', MM(M0[:, 0:256], w_a2_sb[:, :], glowT[:, :]), r=['w_a2', ('glowT', 0), ('glowT', 1)], w=[K6])
            yield
            P.op('act', ACT(Lb[:, :], M0[:, 0:256], AF.Exp, bias=nb_a[:, 0:1], scale=-1.0), r=[K6, 'nb_a'], w=['Lb'])
            P.op('act', ACT(Lb[:, :], Lb[:, :], AF.Ln, bias=1.0), r=['Lb'], w=['Lb'])
            yield
            rm = rmask[:, 1, :] if sample else rmask[:, 0, :]
            P.op('dve', lambda e: e.tensor_tensor_scan(out=cumL[:, :], data0=rm, data1=Lb[:, :], initial=0.0,
                                                       op0=ALU.mult, op1=ALU.add), r=['Lb', 'rmask'], w=['cumL'])
            yield
            yield
            P.op('act', ACT(eq[:, :], cumL[:, :], AF.Exp, scale=-1.0 / 16.0), r=['cumL'], w=['eq'])
            P.op('act', ACT(ek[:, :], cumL[:, :], AF.Exp, scale=1.0 / 16.0), r=['cumL'], w=['ek'])
            yield
            yield
            nchunk = 16 if sample else 2
            cl = 256 // nchunk
            for ch in range(nchunk):
                P.op('dve', TS(eend[:, ch * cl:(ch + 1) * cl], ek[:, ch * cl:(ch + 1) * cl],
                               eq[:, (ch + 1) * cl - 1:(ch + 1) * cl]), r=['eq', 'ek'], w=['eend'])
            P.op('dve', STT(q_decT[:, :], qkT[:, 0, :], 128.0 ** -0.5, eq[:, :], ALU.mult, ALU.mult),
                 r=[('qkT', 0), ('qkT', 1), 'eq'], w=['q_decT'])
            P.op('dve', TT(k_invT[:, :], qkT[:, 1, :], ek[:, :], ALU.mult), r=[('qkT', 0), ('qkT', 1), 'ek'], w=['k_invT'])
            P.op('dve', TT(k_endT[:, :], qkT[:, 1, :], eend[:, :], ALU.mult), r=[('qkT', 0), ('qkT', 1), 'eend'],
                 w=['k_endT'])
            for _ in range(5):
                yield
            for sub in range(2):
                scol = slice(sub * 128, (sub + 1) * 128)
                P.op('pe', MM(M0[:, 256:384], k_invT[:, scol], q_decT[:, scol]), r=['k_invT', 'q_decT'], w=[K6])
                yield
                P.op('dve', TT(attT_bf[:, :], M0[:, 256:384], smk[:, :] if sample else cm[:, :], ALU.mult),
                     r=[K6, 'cm', 'smk'], w=['attT'])
                yield
                if not sample:
                    P.op('pe', TR(M0b[:, 768:896], k_endT[:, scol], ident[:, :]), r=['k_endT', 'ident'], w=[K6])
                    yield
                    P.op('dve', CP('dve', k_end_bf[:, :], M0b[:, 768:896]), r=[K6], w=['k_end'])
                    yield
                    live7[0] += 1
                    P.op('pe', MM(M1[:, 0:256], attT_bf[:, :], v_bf[:, sub, :], start=True, stop=False),
                         r=['attT', ('v_bf', sub)], w=[K7])
                    P.op('pe', MM(M1[:, 0:256], q_decT[:, scol], S_bf[:, :], start=False, stop=True),
                         r=['q_decT', 'S_bf'], w=[K7])
                    P.op('pe', MM(M1[:, 256:512], k_end_bf[:, :], v_bf[:, sub, :]), r=['k_end', ('v_bf', sub)], w=[K7])
                    yield
                    P.op('dve', STT(S_f[:, :], S_f[:, :], eq[:, sub * 128 + 127:sub * 128 + 128], M1[:, 256:512],
                                    ALU.mult, ALU.add), r=['S_f', 'eq', K7], w=['S_f'])
                    P.op('act', CP('act', S_bf[:, :], S_f[:, :]), r=['S_f'], w=['S_bf'])
                    if T == NSUP - 1 and sub == 1:
                        P.dma('sp', DMA(o_S[0, :, :], S_f[:, :]), 'oS', r=['S_f'])
                else:
                    for j in range(8):
                        c16 = slice(sub * 128 + j * 16, sub * 128 + (j + 1) * 16)
                        P.op('pool', CP('pool', qdm[:, j, j * 16:(j + 1) * 16], q_decT[:, c16]), r=['q_decT'], w=['qdm'])
                        P.op('pool', CP('pool', kem[:, j, j * 16:(j + 1) * 16], k_endT[:, c16]), r=['k_endT'], w=['kem'])
                    for j in range(8):
                        P.op('pe', TR(M0b[:, j * 128:(j + 1) * 128], kem[:, j, :], ident[:, :]), r=['kem', 'ident'], w=[K6])
                    P.op('act', CP('act', kend_m[:, :, :], M0b[:, :].rearrange("p (j d) -> p j d", d=128)),
                         r=[K6], w=['kend_m'])
                    live7[0] += 1
                    P.op('pe', MM(M1[:, 0:256], attT_bf[:, :], v_bf[:, sub, :], start=True, stop=False),
                         r=['attT', ('v_bf', sub)], w=[K7])
                    for j in range(8):
                        bb = sub * 8 + j
                        ss = bb % 2
                        P.dma('sp', DMA(S0_f[ss][:, :], s0_d[bb, :, :]), 's0', w=[('S0_f', ss)], nlanes=2)
                        P.op('pool', CP('pool', S0_bf[ss][:, :], S0_f[ss][:, :]), r=[('S0_f', ss)], w=[('S0_bf', ss)])
                        P.op('pe', MM(M1[:, 0:256], qdm[:, j, :], S0_bf[ss][:, :], start=False, stop=(j == 7)),
                             r=['qdm', ('S0_bf', ss)], w=[K7])
                        P.op('pe', MM(M0[:, 0:256], kend_m[:, j, :], v_bf[:, sub, :]), r=['kend_m', ('v_bf', sub)], w=[K6])
                        P.op('dve', STT(Sn_f[ss][:, :], S0_f[ss][:, :],
                                        eq[:, sub * 128 + j * 16 + 15:sub * 128 + j * 16 + 16], M0[:, 0:256],
                                        ALU.mult, ALU.add), r=[('S0_f', ss), 'eq', K6], w=[('Sn_f', ss)])
                        P.dma('sp', DMA(o_S[1 + bb, :, :], Sn_f[ss][:, :]), 'oS', r=[('Sn_f', ss)])
                    for j in range(8):
                        P.op('pool', MS(qdm[:, j, j * 16:(j + 1) * 16], 0.0), r=[], w=['qdm'])
                        P.op('pool', MS(kem[:, j, j * 16:(j + 1) * 16], 0.0), r=[], w=['kem'])
                yield
                P.op('act', ACT(t1[:, :], M1[:, 0:256], AF.Square, accum_out=st4[:, 0:1]), r=[K7], w=['t1', 'st4'])
                P.op('act', ACT(st4[:, 1:2], st4[:, 0:1], AF.Ln, bias=EPS, scale=1.0 / 256.0), r=['st4'], w=['st4'])
                P.op('act', ACT(st4[:, 2:3], st4[:, 1:2], AF.Exp, scale=-0.5), r=['st4'], w=['st4'])
                yield
                yield
                P.op('dve', STT(t2[:, :], M1[:, 0:256], st4[:, 2:3], gng_bc[:, :], ALU.mult, ALU.mult),
                     r=[K7, 'st4', 'gng'], w=['t2'])
                live7[0] -= 1
                P.op('dve', TT(t3[:, :], egg[:, sub, :], gg_f[:, sub, :], ALU.mult), r=[('egg', sub), ('gg_f', sub)], w=['t3'])
                P.op('dve', TT(t2[:, :], t2[:, :], t3[:, :], ALU.mult), r=['t2', 't3'], w=['t2'])
                P.op('dve', TT(mixa[:, sub, :], t2[:, :], sigD[:, sub, 0:256], ALU.mult), r=['t2', ('sigD', slot, sub)],
                     w=[('mixa', slot, sub)])
                yield

        deferred = []

        def flush_deferred(upto=10 ** 9, filler=None):
            last = upto >= 10 ** 9
            while deferred:
                d, fn, needs7 = deferred[0]
                if d > upto:
                    break
                if needs7 and live7[0] > 0:
                    if not last:
                        break
                    while live7[0] > 0:
                        fill(filler, 1)
                deferred.pop(0)
                fn()

        def post_q(T, qs):
            mslot = T % 2
            sigD = sigD2[mslot]
            mixa = mixa2[mslot]
            k4 = ('ps', 4)
            k5 = ('ps', 5)
            P.op('dve', CP('dve', O_sb[:, 0:3, :], ps[:, 4, 0:387].rearrange("p (a c) -> p a c", c=129)), r=[k4], w=['O_sb'])
            P.op('dve', CP('dve', O_sb[:, 3, :], ps[:, 5, 0:129]), r=[k5, 'O_sb'], w=['O_sb'])

            def st_a():
                P.op('dve', RC(rden[:, 0:4], O_sb[:, 0:4, 128]), r=['O_sb'], w=['rden'])
                for h in range(2):
                    k1 = h * 2
                    k2 = h * 2 + 1
                    P.op('dve', TS(od2[:, h, :], O_sb[:, k2, 0:128], rden[:, k2:k2 + 1], nlam[:, 0:1], ALU.mult, ALU.mult),
                         r=['O_sb', 'rden', 'nlam'], w=[('od2', h)])
                    P.op('dve', STT(od[:, h, :], O_sb[:, k1, 0:128], rden[:, k1:k1 + 1], od2[:, h, :], ALU.mult, ALU.add),
                         r=['O_sb', 'rden', ('od2', h)], w=[('od', h)])

            def st_b():
                for h in range(2):
                    P.op('act', ACT(od2[:, h, :], od[:, h, :], AF.Square, accum_out=ssq[:, h:h + 1]), r=[('od', h)],
                         w=[('od2', h), ('ssq', h)])
                P.op('act', ACT(ssq[:, 2:4], ssq[:, 0:2], AF.Ln, bias=EPS, scale=1.0 / 128.0), r=[('ssq', 0), ('ssq', 1)],
                     w=['ssq2'])
                P.op('act', ACT(ssq[:, 2:4], ssq[:, 2:4], AF.Exp, scale=-0.5), r=['ssq2'], w=['ssq2'])

            def st_c():
                for h in range(2):
                    P.op('dve', STT(mix_f[:, qs, h * 128:(h + 1) * 128], od[:, h, :], ssq[:, 2 + h:3 + h],
                                    dng_bc[:, h * 128:(h + 1) * 128], ALU.mult, ALU.mult), r=[('od', h), 'ssq2', 'dng'],
                         w=[('mix_f', qs)])
                P.op('dve', TT(mix_f[:, qs, :], mix_f[:, qs, :], sigD[:, qs, 256:512], ALU.mult),
                     r=[('mix_f', qs), ('sigD', mslot, qs)], w=[('mix_f', qs)])
                P.op('dve', TT(mix_bf[:, qs, :], mix_f[:, qs, :], mixa[:, qs, :], ALU.add),
                     r=[('mix_f', qs), ('mixa', mslot, qs)], w=[('mix_bf', qs)])
            deferred.append((0, st_a, False))
            deferred.append((2, st_b, False))
            deferred.append((4, st_c, False))
            deferred.append((6, lambda: post_q_b(T, qs), True))

        def post_q_b(T, qs):
            mslot = T % 2
            for c in range(2):
                P.op('pe', TR(M1b[:, c * 128:(c + 1) * 128], mix_bf[:, qs, c * 128:(c + 1) * 128], ident[:, :]),
                     r=[('mix_bf', qs), 'ident'], w=[K7])
            P.op('act', CP('act', mixT[mslot][:, :, qs * 128:(qs + 1) * 128], M1b[:, 0:256].rearrange("p (c t) -> p c t", t=128)),
                 r=[K7], w=[('mixT', mslot)])
            if qs == 1:
                post_final(T)

        def post_final(T):
            mslot = T % 2
            ck = T // SPC
            c0 = (T % SPC) * 256
            P.dma('sp', DMA(mz[ck, :, c0:c0 + 256].rearrange("(c p) t -> p c t", p=128), mixT[mslot][:, :, :]),
                  'omix', r=[('mixT', mslot)], w=[('mixedT', T)])
            if T == NSUP or (T % SPC) == SPC - 1:
                cc_pending.append(ck)
            if CCP and T >= T_CC and T < NSUP:
                if not cc_ops:
                    issue_bulk(100)
                    extra = [o for ln, o in P.lane_last.items() if ln[0] == 'wdc']
                else:
                    extra = []
                for _ in range(2):
                    if not cc_pending:
                        break
                    k_ = cc_pending.pop(0)
                    cc_ops.append(P.dma('pool', mk_cc(k_), 'cc', deps=extra,
                                        r=[('mixedT', t) for t in range(k_ * SPC, min((k_ + 1) * SPC, NSUP + 1))],
                                        w=[('gath', k_)], nlanes=1, inc=1))

        cc_ops = []
        cc_pending = []
        CCP = os.environ.get('KCCP', '1') == '1'
        T_CC = min(27, max(1, NSUP - 3))

        def mk_cc(ck):
            return lambda e: e.collective_compute("AllGather", ALU.bypass, replica_groups=[[0, 1, 2, 3], [4, 5, 6, 7]],
                                                  ins=[mz[ck, :, :]], outs=[gv[ck + 1, :, :]])

        def fill(gen, n=1):
            if gen is None:
                return
            for _ in range(n):
                try:
                    next(gen)
                except StopIteration:
                    return

        def drain(gen):
            if gen is None:
                return
            for _ in gen:
                pass

        kctr = [0]

        def attention_prompt(T, qs, filler):
            slot = T % 2
            qi = 2 * T + qs
            qc = slice(qs * 128, (qs + 1) * 128)
            k0 = kctr[0]
            kctr[0] += qi + 1

            def rec_S(j):
                x = (k0 + j) % 2
                for h in range(2):
                    for m in range(2):
                        P.op('pe', MM(pSbuf(x)[:, m, h * 128:(h + 1) * 128], kdT[64 * m:64 * m + 64, h, j * 128:(j + 1) * 128],
                                      qdT[slot][64 * m:64 * m + 64, h, qc]),
                             r=[('kdT', j), ('qdT', slot)], w=pSk(x))

            rec_S(0)
            for j in range(qi + 1):
                if j + 1 <= qi:
                    rec_S(j + 1)
                k = k0 + j
                x = k % 2
                pt = PT[k % 4]
                pk = ('PT', k % 4)
                P.op('act', ACT(pt[:, :, :], pSbuf(x)[:, :, 0:256], AF.Exp, scale=0.125), r=pSk(x), w=[pk])
                if j == qi:
                    P.op('dve', MS(pt[64:128, :, :].rearrange("p m (h q) -> p m h q", q=128)[:, :, :, 0:64], 0.0), r=[], w=[pk])
                fill(filler, 1)
                flush_deferred(j if j < qi else 10 ** 9, filler)
                for h in range(2):
                    for m in range(2):
                        kacc = h * 2 + m
                        P.op('pe', MM(pOacc(kacc), pt[:, m, h * 128:(h + 1) * 128], V_sb[:, j, h, 0:129],
                                      start=(j == 0 and kacc in (0, 3)), stop=(j == qi), skip=True),
                             r=[pk, ('V', j)], w=[pOk(kacc)])

        def attention_sample(sub):
            T = NSUP
            slot = T % 2
            for jj in range(8):
                bb = sub * 8 + jj
                for h in range(2):
                    it = bb * 2 + h
                    sl = it % 2
                    x = it % 2
                    P.dma('sp', DMA(Kst[sl], ckT[bb, h, :, :]), 'kc', w=[('Kst', sl)], nlanes=2)
                    P.dma('sp', DMA(Vst[sl], cv[bb, h, :, :, :]), 'vc', w=[('Vst', sl)], nlanes=2)
                    P.op('act', CP('act', KcT[sl], Kst[sl]), r=[('Kst', sl)], w=[('KcT', sl)])
                    P.op('dve', CP('dve', Vc[sl][:, :, 0:128], Vst[sl]), r=[('Vst', sl), ('Vc1', sl)], w=[('Vc', sl)])
                    qc = slice(bb * 16, bb * 16 + 16)
                    for kt in range(9):
                        for m in range(2):
                            if kt < 8:
                                lhs = KcT[sl][64 * m:64 * m + 64, kt * 128:(kt + 1) * 128]
                                rk = [('KcT', sl)]
                            else:
                                lhs = kdTs[64 * m:64 * m + 64, h, sub * 128:(sub + 1) * 128]
                                rk = [('kdTs', sub)]
                            P.op('pe', MM(pSbuf(x)[:, m, kt * 16:(kt + 1) * 16], lhs, qdT[slot][64 * m:64 * m + 64, h, qc]),
                                 r=rk + [('qdT', slot)], w=pSk(x))
                    P.op('act', ACT(PTs[:, :, :, jj * 16:(jj + 1) * 16],
                                    pSbuf(x)[:, :, 0:144].rearrange("p m (k q) -> p m k q", q=16), AF.Exp, scale=0.125),
                         r=pSk(x), w=['PTs'])
                    P.op('dve', TT(PTs[:, :, 8, jj * 16:(jj + 1) * 16], PTs[:, :, 8, jj * 16:(jj + 1) * 16],
                                   bmask[:, jj, :, :], ALU.mult), r=['PTs', 'bmask'], w=['PTs'])
                    for kt in range(9):
                        for m in range(2):
                            kacc = h * 2 + m
                            rhs = Vc[sl][:, kt, 0:129] if kt < 8 else Vs_sb[:, sub, h, 0:129]
                            rk = [('Vc', sl)] if kt < 8 else [('Vs', sub)]
                            P.op('pe', MM(pOacc(kacc), PTs[:, m, kt, :], rhs,
                                          start=(jj == 0 and kt == 0 and kacc in (0, 3)),
                                          stop=(jj == 7 and kt == 8), skip=True),
                                 r=['PTs'] + rk, w=[pOk(kacc)])
                    P.op('pool', MS(PTs[:, :, :, jj * 16:(jj + 1) * 16], 0.0), r=[], w=['PTs'])

        def done():
            P.emit()
            return nc, dict(n_sems=P.n_sems, n_ops=P.n_ops)

        if stage == 1:
            issue_bulk(100)
            return done()
        if stage == 2:
            fill(prep(0), int(os.environ.get("KSTEPS", "100")))
            return done()
        drain(prep(0))
        if stage == 3:
            for qs in range(2):
                attention_prompt(0, qs, None)
                post_q(0, qs)
            flush_deferred()
            return done()
        for T in range(NSUP):
            nxt = prep(T + 1)
            for qs in range(2):
                attention_prompt(T, qs, nxt)
                post_q(T, qs)
            drain(nxt)
            if T + 2 <= NSUP:
                load_x(T + 2)
            if T >= 3:
                issue_bulk(1)
        issue_bulk(100)
        flush_deferred()
        if stage == 4:
            return done()
        P.barrier()
        P.op('pool', MS(PTs[:, :, :, :], 0.0), w=['PTs'])
        for sl in range(2):
            P.op('pool', MS(Vc[sl][:, :, 128:130], 1.0), w=[('Vc1', sl)])
        for sub in range(2):
            attention_sample(sub)
            post_q(NSUP, sub)
            flush_deferred()

        if debug:
            bd = P.barrier()
            P.dma('sp', DMA(o_mixT[:, :, :], mz[:, :, :]), 'dbg', deps=bd)
        if stage == 5:
            return done()

        bar0 = P.barrier()
        for ck in cc_pending:
            cc_ops.append(P.dma('pool', mk_cc(ck), 'cc', deps=bar0, w=[('gath', ck)], nlanes=1, inc=1))
        cc = cc_ops[-1]
        bar = P.barrier()

        if stage == 6:
            return done()
        ar.off = persist_off
        w_up_sb = ar.alloc([KC, 2 * DFF], BF16)
        w_out_sb = ar.alloc([KC, D], BF16)
        wd_sb = [ar.alloc([D], BF16) for _ in range(4)]
        lnbc = ar.alloc([4, D], F32)
        x_sb = ar.alloc([2, D], F32)
        zs = ar.alloc([2, D], F32)
        x1_f = ar.alloc([2, D], F32)
        zs2 = ar.alloc([2, D], F32)
        x1_bf = ar.alloc([2, D], BF16)
        mixT_sb = [ar.alloc([KC, 256], BF16) for _ in range(2)]
        x1T_sb = ar.alloc([KC, 256], BF16)
        a_ext = [ar.alloc([258], F32) for _ in range(4)]
        cacc = [ar.alloc([256], F32) for _ in range(4)]
        gl = [ar.alloc([256], F32) for _ in range(4)]
        h_bf = [ar.alloc([256], BF16) for _ in range(4)]
        halo_all = ar.alloc([NCH, 2], F32)
        fc_sb = ar.alloc([NCH, 5, 2], F32)
        convp = ar.alloc([NCH, 4], F32)
        cfc = ar.alloc([NCH, 4, 2], F32)
        flag = ar.alloc([1], F32)
        bst = ar.alloc([2, 12], F32)
        mv = ar.alloc([2, 4], F32)
        p3_total = ar.off

        def pZ(sub):
            return ps[:, 2 * sub:2 * sub + 2, :] if sub == 0 else ps[:, 4:6, :]

        def pZkeys(sub):
            return [('ps', 0), ('ps', 1)] if sub == 0 else [('ps', 4), ('ps', 5)]

        def pD(sub):
            return ps[:, 2 + 2 * sub:4 + 2 * sub, :]

        def pDkeys(sub):
            return [('ps', 2 + 2 * sub), ('ps', 3 + 2 * sub)]

        def pU(i):
            return ps[:, 6 + i, :]

        for kc in range(KC):
            P.dma('sp', DMA(w_out_sb[:, kc, :], wout_bf[kc * 128:(kc + 1) * 128, :]), 'wl3', w=[('w_out', kc)], deps=bar)
        wupv = wup_bf.ap().rearrange("(kc p) n -> p kc n", p=128)
        for g6 in range(4):
            c0 = g6 * 768
            cw = min(768, DFF - c0)
            for part in range(2):
                P.dma('sp', DMA(w_up_sb[:, :, part * DFF + c0:part * DFF + c0 + cw], wupv[:, :, part * DFF + c0:part * DFF + c0 + cw]),
                      'wl3', w=[('w_up', g6, part)], deps=bar)
        P.dma('sp', DMA(lnbc[:, :, :], lnbc_d[:, :, :]), 'cld', w=['lnbc'], deps=bar)
        P.dma('sp', DMA(convp[:, :, :], convp_d[:, :, :]), 'cld', w=['convp'], deps=bar)
        P.dma('sp', DMA(cfc[:, :, :, :], cfc_d[:, :, :, :]), 'cld', w=['cfc'], deps=bar)
        P.dma('sp', DMA(flag[:, :], flag_d[:, :]), 'cld', w=['flag'], deps=bar)

        gvv = gv.rearrange("c (kc p) t -> p (c kc) t", p=128)
        wdv = wdn_bf.ap()
        tile_ctr = [0]
        jreg = {}
        chunk_ctr = [0]

        def ffn_tile(kind, row0, n, nb, L, tix, last_main=False):
            ti = tile_ctr[0]
            tile_ctr[0] += 1
            ms_ = min(128, n)
            nsub = (n + 127) // 128
            mslot = ti % 2

            def ld_mix(e):
                if 'j' not in jreg:
                    jreg['j'] = e.snap(e.partition_id() % 4, min_val=0, max_val=3)
                j = jreg['j']
                if kind == 'halo':
                    src = gvv[:, bass.ds(j * (CPQ * 8), 8), CK - 2:CK]
                elif kind == 'main':
                    cko = 1 + (tix * 256) // CK
                    c0 = (tix * 256) % CK
                    src = gvv[:, bass.ds(j * (CPQ * 8) + cko * 8, 8), c0:c0 + 256]
                else:
                    src = gvv[:, (NCK + 1) * 8:(NCK + 2) * 8, bass.ds(j * 64, 64)]
                return e.dma_start(out=mixT_sb[mslot][:, :, 0:n], in_=src)
            P.dma('sp', ld_mix, 'mld', w=[('mixT_sb', mslot)], deps=[cc], nlanes=2)
            for sub in range(nsub):
                P.dma('sp', DMA(x_sb[0:ms_, sub, :], x3[row0 + sub * 128:row0 + sub * 128 + ms_, :]), 'xld3',
                      w=[('x_sb', sub)], nlanes=2)
            for sub in range(nsub):
                z = pZ(sub)
                for half in range(2):
                    for kc in range(KC):
                        P.op('pe', MM(z[0:ms_, half, :], mixT_sb[mslot][:, kc, sub * 128:sub * 128 + ms_],
                                      w_out_sb[:, kc, half * 512:(half + 1) * 512], start=(kc == 0), stop=(kc == KC - 1)),
                             r=[('mixT_sb', mslot), ('w_out', kc)], w=pZkeys(sub))
                P.op('dve', STT(zs[0:ms_, sub, :], x_sb[0:ms_, sub, :], ALPHA, z[0:ms_, :, :].rearrange("p a b -> p (a b)"),
                                ALU.mult, ALU.add), r=[('x_sb', sub)] + pZkeys(sub), w=[('zs', sub)])
                layer_norm(sub, ms_, zs, 0, x1_f, ('x1_f', sub))
                P.op('act', CP('act', x1_bf[0:ms_, sub, :], x1_f[0:ms_, sub, :]), r=[('x1_f', sub)], w=[('x1_bf', sub)])
                ub = ps[:, 6 + sub, :].bitcast(BF16)
                for kc in range(KC):
                    P.op('pe', TR(ub[:, kc * 128:kc * 128 + ms_], x1_bf[0:ms_, sub, kc * 128:(kc + 1) * 128],
                                  ident[0:ms_, 0:ms_]), r=[('x1_bf', sub), 'ident'], w=[('ps', 6 + sub)])
                P.op('dve', CP('dve', x1T_sb[:, :, sub * 128:sub * 128 + ms_],
                               ub[:, :].rearrange("p (k t) -> p k t", t=128)[:, :, 0:ms_]),
                     r=[('ps', 6 + sub)], w=[('x1T', sub)])
            x1Tk = [('x1T', s) for s in range(nsub)]
            UB = [0, 1, 6, 7] if os.environ.get('KUB', '4') == '4' else [6, 7, 6, 7]
            LAG = int(os.environ.get('KLAG', '2'))
            base = chunk_ctr[0]
            chunk_ctr[0] += NCH

            def rec_up(c):
                ci = base + c
                u = ps[:, UB[ci % 4], :]
                ukey = ('ps', UB[ci % 4])
                for part in range(2):
                    for kc in range(KC):
                        P.op('pe', MM(u[:, part * 256:part * 256 + n],
                                      w_up_sb[:, kc, part * DFF + c * 128:part * DFF + (c + 1) * 128],
                                      x1T_sb[:, kc, 0:n], start=(kc == 0), stop=(kc == KC - 1)),
                             r=x1Tk + [('w_up', c // 6, part)], w=[ukey])
                if kind == 'halo':
                    P.op('dve', TS(halo_all[:, c, :], u[:, 0:2], flag[:, 0:1]), r=[ukey, 'flag'], w=[('halo', c)])
                    return
                ws = ci % 4
                P.dma('sp', DMA(wd_sb[ws][:, :], wdv[c * 128:(c + 1) * 128, :]), 'wdl', r=[('wdn', c)], w=[('wd_sb', ws)])
                es_ = ci % 4
                ae = a_ext[es_][:, 0:nb * (L + 2)].rearrange("p (b l) -> p b l", l=L + 2)
                aek = ('a_ext', es_)
                P.op('act', CP('act', ae[:, :, 2:L + 2], u[:, 0:n].rearrange("p (b l) -> p b l", l=L)), r=[ukey], w=[aek])
                if kind == 'main':
                    P.op('pool', CP('pool', ae[:, 0, 0:2], halo_all[:, c, :]), r=[('halo', c)], w=[aek])
                else:
                    P.op('pool', CP('pool', ae[:, :, 0:2], cfc[:, c, :, :]), r=['cfc'], w=[aek])
                ca = cacc[es_][:, 0:n].rearrange("p (b l) -> p b l", l=L)
                cak = ('cacc', es_)
                P.op('dve', TS(ca, ae[:, :, 0:L], convp[:, c, 0:1], convp[:, c, 3:4], ALU.mult, ALU.add),
                     r=[aek, 'convp'], w=[cak])
                P.op('dve', STT(ca, ae[:, :, 1:L + 1], convp[:, c, 1:2], ca, ALU.mult, ALU.add), r=[aek, cak, 'convp'], w=[cak])
                P.op('dve', STT(ca, ae[:, :, 2:L + 2], convp[:, c, 2:3], ca, ALU.mult, ALU.add), r=[aek, cak, 'convp'], w=[cak])
                P.op('act', ACT(gl[es_][:, 0:n], cacc[es_][:, 0:n], AF.Gelu_apprx_tanh), r=[cak], w=[('gl', es_)])
                P.op('dve', TT(h_bf[es_][:, 0:n], gl[es_][:, 0:n], u[:, 256:256 + n], ALU.mult), r=[('gl', es_), ukey],
                     w=[('h_bf', es_)])
                if kind == 'main':
                    P.op('pool', CP('pool', halo_all[:, c, :], ae[:, 0, L:L + 2]), r=[aek], w=[('halo', c)])
                    if last_main:
                        P.op('pool', CP('pool', fc_sb[:, c, 0, :], ae[:, 0, L:L + 2]), r=[aek], w=[('fc', c)])
                else:
                    P.op('pool', CP('pool', fc_sb[:, c, 1:5, :], ae[:, :, L:L + 2]), r=[aek], w=[('fc', c)])

            def rec_down(c):
                ci = base + c
                ws = ci % 4
                es_ = ci % 4
                for sub in range(nsub):
                    dd = pD(sub)
                    for half in range(2):
                        P.op('pe', MM(dd[0:ms_, half, :], h_bf[es_][:, sub * 128:sub * 128 + ms_],
                                      wd_sb[ws][:, half * 512:(half + 1) * 512], start=(c == 0), stop=(c == NCH - 1)),
                             r=[('h_bf', es_), ('wd_sb', ws)], w=pDkeys(sub))

            for cc_ in range(NCH + LAG):
                if cc_ < NCH:
                    rec_up(cc_)
                if kind != 'halo' and cc_ >= LAG:
                    rec_down(cc_ - LAG)
            if kind == 'halo':
                return
            orow = row0 - 2
            for sub in range(nsub):
                dd = pD(sub)
                P.op('dve', STT(zs2[0:ms_, sub, :], x1_f[0:ms_, sub, :], ALPHA, dd[0:ms_, :, :].rearrange("p a b -> p (a b)"),
                                ALU.mult, ALU.add), r=[('x1_f', sub)] + pDkeys(sub), w=[('zs2', sub)])
                layer_norm(sub, ms_, zs2, 2, zs2, ('zs2', sub), skey=('zs2', sub))
                P.dma('sp', DMA(y3[orow + sub * 128:orow + sub * 128 + ms_, :], zs2[0:ms_, sub, :]), 'oy', r=[('zs2', sub)])

        def layer_norm(sub, ms_, src, gi, dst, dkey, skey=None):
            skey = skey or ('zs', sub)
            for hf in range(2):
                P.op('dve', lambda e, hf=hf: e.bn_stats(out=bst[0:ms_, sub, hf * 6:(hf + 1) * 6],
                                                        in_=src[0:ms_, sub, hf * 512:(hf + 1) * 512]),
                     r=[skey], w=[('bst', sub)])
            P.op('dve', lambda e: e.bn_aggr(out=mv[0:ms_, sub, 0:2], in_=bst[0:ms_, sub, :]), r=[('bst', sub)], w=[('mv', sub)])
            P.op('act', ACT(mv[0:ms_, sub, 2:3], mv[0:ms_, sub, 1:2], AF.Sqrt, bias=EPS, scale=1.0), r=[('mv', sub)],
                 w=[('mv', sub)])
            P.op('dve', RC(mv[0:ms_, sub, 3:4], mv[0:ms_, sub, 2:3]), r=[('mv', sub)], w=[('mv', sub)])
            P.op('dve', TS(src[0:ms_, sub, :], src[0:ms_, sub, :], mv[0:ms_, sub, 0:1], mv[0:ms_, sub, 3:4],
                           ALU.subtract, ALU.mult), r=[skey, ('mv', sub)], w=[skey])
            P.op('dve', TT(src[0:ms_, sub, :], src[0:ms_, sub, :], lnbc[0:ms_, gi, :], ALU.mult), r=[skey, 'lnbc'], w=[skey])
            P.op('dve', TT(dst[0:ms_, sub, :], src[0:ms_, sub, :], lnbc[0:ms_, gi + 1, :], ALU.add), r=[skey, 'lnbc'],
                 w=[dkey] if dkey != skey else [skey])

        ffn_tile('halo', 0, 2, 1, 2, 0)
        nmain = TQ // 256
        for i in range(nmain):
            ffn_tile('main', 2 + i * 256, 256, 1, 256, i, last_main=(i == nmain - 1))
        ffn_tile('sample', 2 + TQ, 64, 4, 16, 0)
        P.dma('sp', DMA(o_fc[:, :], fc_sb[:, :, :, :].rearrange("p c s t -> p (c s t)")), 'ofc',
              r=[('fc', c) for c in range(NCH)])

        P.emit()
        info = dict(p1_words=p1_total, p3_words=p3_total, n_sems=P.n_sems, n_ops=P.n_ops)
    return nc, info


def host_inputs(inp, TP=8192):
    f = np.float32
    xp = np.asarray(inp["x_prompt"], f)[:, :TP]
    xs = np.asarray(inp["x_sample"], f)
    w_in = np.asarray(inp["w_in"], f)[0]
    ck = np.asarray(inp["cache_diff_k"], f)[0]
    cvv = np.asarray(inp["cache_diff_v"], f)[0]
    sg = np.asarray(inp["state_gla"], f)[0]
    cf = np.asarray(inp["cache_ffn_conv"], f)[0]
    TQ = TP // 4
    ident = np.eye(128, dtype=f)
    s_idx = np.arange(128)
    cm = (s_idx[:, None] <= s_idx[None, :]).astype(f)
    smk = cm * (s_idx[:, None] // 16 == s_idx[None, :] // 16).astype(f)
    rmask = np.ones((128, 2, 256), f)
    rmask[:, 0, 0::128] = 0.0
    rmask[:, 1, 0::16] = 0.0
    bmask = np.zeros((128, 8, 2, 16), f)
    for j in range(8):
        bmask[16 * j:16 * (j + 1), j] = 1.0
    lnbc = np.stack([np.broadcast_to(np.asarray(inp[k], f)[0], (128, D)) for k in ("ln1_g", "ln1_b", "ln2_g", "ln2_b")], axis=1)
    convw = np.asarray(inp["conv_w"], f)[0]
    convb = np.asarray(inp["conv_b"], f)[0]
    convp = np.concatenate([convw, convb[None]], axis=0).reshape(4, NCH, 128).transpose(2, 1, 0)
    lamv = np.broadcast_to(np.stack([np.asarray(inp[k], f)[0] for k in ("lam_q1", "lam_k1", "lam_q2", "lam_k2")], 0), (128, 4, 64))
    gng_bc = np.broadcast_to(np.asarray(inp["gla_norm_g"], f)[0], (128, 256))
    dng = np.asarray(inp["diff_norm_g"], f)[0]
    dng_bc = np.broadcast_to(np.concatenate([dng, dng]), (128, 256))
    offs = [0, 512, 1024, 2048, 3072, 3088, 4112, 5136, 6160, 7184]
    w_out = np.ascontiguousarray(np.asarray(inp["w_out"], f)[0])
    w_up = np.ascontiguousarray(np.asarray(inp["w_up"], f)[0])
    w_down = np.ascontiguousarray(np.asarray(inp["w_down"], f)[0])
    maps = []
    for c in range(NCORES):
        b, g = c // 4, c % 4
        j = g
        sb = slice(16 * b, 16 * b + 16)
        xT1 = np.concatenate([xp[b].T, xs[sb].reshape(256, D).T], axis=1)
        cols = [w_in[:, offs[0] + g * 128:offs[0] + (g + 1) * 128],
                w_in[:, offs[1] + g * 128:offs[1] + (g + 1) * 128],
                w_in[:, offs[5] + g * 256:offs[5] + (g + 1) * 256],
                w_in[:, offs[6] + g * 256:offs[6] + (g + 1) * 256],
                w_in[:, offs[7] + g * 256:offs[7] + (g + 1) * 256],
                w_in[:, offs[2] + g * 256:offs[2] + (g + 1) * 256],
                w_in[:, offs[3] + g * 256:offs[3] + (g + 1) * 256],
                w_in[:, offs[8] + g * 256:offs[8] + (g + 1) * 256],
                w_in[:, offs[9] + g * 256:offs[9] + (g + 1) * 256],
                w_in[:, offs[4]:offs[4] + 16]]
        w_in1 = np.concatenate(cols, axis=1)
        ckT = ck[sb][:, :, 2 * g:2 * g + 2].reshape(16, 1024, 2, 128).transpose(0, 2, 3, 1)
        cvh = cvv[sb][:, :, 2 * g:2 * g + 2].reshape(16, 8, 128, 2, 128).transpose(0, 3, 2, 1, 4)
        x3 = np.zeros((2 + TQ + 64, D), f)
        if j > 0:
            x3[0:2] = xp[b, TQ * j - 2:TQ * j]
        x3[2:2 + TQ] = xp[b, TQ * j:TQ * (j + 1)]
        x3[2 + TQ:] = xs[4 * c:4 * c + 4].reshape(64, D)
        cfc = cf[4 * c:4 * c + 4].reshape(4, 2, NCH, 128).transpose(3, 2, 0, 1)
        m = dict(
            xT1=xT1, w_in1=w_in1, w_a2g=np.asarray(inp["w_a2"], f)[0][:, g * 128:(g + 1) * 128],
            b_ag=np.asarray(inp["b_a"], f)[0][g * 128:(g + 1) * 128].reshape(128, 1),
            gng_bc=gng_bc, dng_bc=dng_bc, lamv=lamv, ckT=ckT, cv=cvh, s0=sg[sb, g], ident=ident, cm=cm, smk=smk,
            rmask=rmask, bmask=bmask, x3=x3, w_out=w_out, w_up=w_up, w_down=w_down, lnbc=lnbc, convp=convp, cfc=cfc,
            flag=np.full((128, 1), 1.0 if j > 0 else 0.0, f))
        maps.append({k: np.ascontiguousarray(v, dtype=f) for k, v in m.items()})
    return maps


def assemble(res, TP=8192):
    f = np.float32
    TQ = TP // 4
    yp = np.zeros((2, TP, D), f)
    ys = np.zeros((32, 16, D), f)
    pk = np.zeros((1, 2, TP, 8, 2, 64), f)
    pv = np.zeros((1, 2, TP, 8, 128), f)
    pS = np.zeros((1, 2, 4, 128, 256), f)
    pc = np.zeros((1, 2, 2, DFF), f)
    sk = np.zeros((1, 32, 16, 8, 2, 64), f)
    sv = np.zeros((1, 32, 16, 8, 128), f)
    sS = np.zeros((1, 32, 4, 128, 256), f)
    sc = np.zeros((1, 32, 2, DFF), f)
    for c in range(NCORES):
        r = res[c]
        b, g = c // 4, c % 4
        j = g
        okv = r["o_kv"]
        pk[0, b, :, 2 * g:2 * g + 2] = okv[:TP, 0:256].reshape(TP, 2, 2, 64)
        pv[0, b, :, 2 * g:2 * g + 2] = okv[:TP, 256:512].reshape(TP, 2, 128)
        sk[0, 16 * b:16 * b + 16, :, 2 * g:2 * g + 2] = okv[TP:, 0:256].reshape(16, 16, 2, 2, 64)
        sv[0, 16 * b:16 * b + 16, :, 2 * g:2 * g + 2] = okv[TP:, 256:512].reshape(16, 16, 2, 128)
        pS[0, b, g] = r["o_S"][0]
        sS[0, 16 * b:16 * b + 16, g] = r["o_S"][1:]
        yp[b, TQ * j:TQ * (j + 1)] = r["y3"][:TQ]
        ys[4 * c:4 * c + 4] = r["y3"][TQ:].reshape(4, 16, D)
        fc = r["o_fc"].reshape(128, NCH, 5, 2).transpose(2, 3, 1, 0).reshape(5, 2, DFF)
        if j == 3:
            pc[0, b] = fc[0]
        sc[0, 4 * c:4 * c + 4] = fc[1:]
    return (yp, ys, pk, pv, pS, pc, sk, sv, sS, sc)


_CACHE = {}


def kernel(**inputs):
    TP = 8192
    if TP not in _CACHE:
        _CACHE[TP] = build(TP)[0]
    nc = _CACHE[TP]
    maps = host_inputs(inputs, TP)
    out = run_bass_kernel_spmd(nc, maps, core_ids=list(range(NCORES)))
    return assemble(out.results, TP)
```

```python
import math
import os
from contextlib import ExitStack

import numpy as np

import concourse.bass as bass
import concourse.mybir as mybir
from concourse.bass_utils import run_bass_kernel_spmd

F32 = mybir.dt.float32
BF16 = mybir.dt.bfloat16
AF = mybir.ActivationFunctionType
ALU = mybir.AluOpType

D = 1024
KC = 8
DFF = 2816
NCH = 22
EPS = 1e-5
ALPHA = 2.0 ** 0.25
LAM_INIT = 0.8 - 0.6 * math.exp(-0.3 * 0)
NCORES = 8
WIN = 2064

ENGS = ['pe', 'act', 'dve', 'pool', 'sp']


class Op:
    __slots__ = ('eng', 'fn', 'deps', 'lane', 'inc', 'sig', 'semkey', 'val')

    def __init__(self, eng, fn, deps, lane=None, inc=1):
        self.eng = eng
        self.fn = fn
        self.deps = deps
        self.lane = lane
        self.inc = inc
        self.sig = lane is not None
        self.semkey = None
        self.val = None


class Prog:
    ROLL = int(os.environ.get("KROLL", "8000"))

    def __init__(self, nc, es):
        self.nc = nc
        self.es = es
        self.q = {e: [] for e in ENGS}
        self.last_w = {}
        self.readers = {}
        self.lane_last = {}
        self.lane_eng = {}
        self.pool_ctr = {}

    def _auto(self, r, w):
        deps = []
        for k in r:
            lw = self.last_w.get(k)
            if lw is not None:
                deps.append(lw)
        for k in w:
            lw = self.last_w.get(k)
            if lw is not None:
                deps.append(lw)
            deps.extend(self.readers.get(k, ()))
        return deps

    def _reg(self, o, r, w):
        for k in r:
            self.readers.setdefault(k, []).append(o)
        for k in w:
            self.last_w[k] = o
            self.readers[k] = []

    @staticmethod
    def _excl(r, w):
        pr = [k for k in r if isinstance(k, tuple) and k[0] == 'ps']
        if pr:
            r = [k for k in r if not (isinstance(k, tuple) and k[0] == 'ps')]
            w = list(w) + [k for k in pr if k not in w]
        return r, w

    def op(self, eng, fn, r=(), w=(), deps=()):
        r, w = self._excl(r, w)
        d = [x for x in deps if x is not None] + self._auto(r, w)
        o = Op(eng, fn, d)
        self.q[eng].append(o)
        self._reg(o, r, w)
        return o

    def dma(self, eng, fn, pool, r=(), w=(), deps=(), nlanes=4, inc=16):
        i = self.pool_ctr.get(pool, 0)
        self.pool_ctr[pool] = i + 1
        lane = (pool, i % nlanes)
        assert self.lane_eng.setdefault(lane, eng) == eng, lane
        r, w = self._excl(r, w)
        d = [x for x in deps if x is not None] + self._auto(r, w)
        prev = self.lane_last.get(lane)
        if prev is not None:
            d.append(prev)
        o = Op(eng, fn, d, lane=lane, inc=inc)
        self.lane_last[lane] = o
        self.q[eng].append(o)
        self._reg(o, r, w)
        return o

    def barrier(self, exclude=()):
        lasts = [self.q[e][-1] for e in ENGS if self.q[e] and not (self.q[e][-1].lane is not None and self.q[e][-1].lane[0] in exclude)]
        lasts += [o for ln, o in self.lane_last.items() if ln[0] not in exclude]
        res = []
        for e in ENGS:
            o = Op(e, None, list(lasts))
            self.q[e].append(o)
            res.append(o)
        self.last_w = {}
        self.readers = {}
        return res

    def emit(self):
        nc = self.nc
        for e in ENGS:
            for o in self.q[e]:
                for d in o.deps:
                    if d.eng == 'pe' and o.eng == 'pe' and d.lane is None:
                        continue
                    d.sig = True
        keys = []
        lane_cnt = {}
        for e in ENGS:
            cnt = 0
            idx = 0
            for o in self.q[e]:
                if o.lane is not None:
                    c = lane_cnt.get(o.lane, 0) + o.inc
                    lane_cnt[o.lane] = c
                    o.semkey = ('lane',) + tuple(o.lane)
                    o.val = c
                elif o.sig and o.fn is not None:
                    cnt += 1
                    if cnt > self.ROLL:
                        idx += 1
                        cnt = 1
                    o.semkey = ('done', e, idx)
                    o.val = cnt
                if o.semkey is not None and o.semkey not in keys:
                    keys.append(o.semkey)
        sems = {}
        for i, k in enumerate(keys):
            sems[k] = self.es.enter_context(nc.semaphore("s%d" % i))
        self.n_sems = len(keys)
        self.n_ops = {e: len(self.q[e]) for e in ENGS}
        block = self.es.enter_context(nc.Block())

        def runner(eng):
            def run(h):
                waited = {}
                for o in self.q[eng]:
                    for d in o.deps:
                        if d.fn is None:
                            continue
                        if d.eng == 'pe' and eng == 'pe' and d.lane is None:
                            continue
                        if waited.get(d.semkey, 0) < d.val:
                            h.wait_ge(sems[d.semkey], d.val)
                            waited[d.semkey] = d.val
                    if o.fn is None:
                        continue
                    ins = o.fn(h)
                    if o.sig:
                        ins.then_inc(sems[o.semkey], o.inc)
                if eng == 'sp':
                    for ln, c in lane_cnt.items():
                        k = ('lane',) + tuple(ln)
                        if waited.get(k, 0) < c:
                            h.wait_ge(sems[k], c)
            return run

        block.tensor(runner('pe'))
        block.scalar(runner('act'))
        block.vector(runner('dve'))
        block.gpsimd(runner('pool'))
        block.sync(runner('sp'))


def MM(out, lhsT, rhs, start=True, stop=True, skip=False):
    if skip:
        return lambda e: e.matmul(out, lhsT=lhsT, rhs=rhs, start=start, stop=stop, skip_group_check=True)
    return lambda e: e.matmul(out, lhsT=lhsT, rhs=rhs, start=start, stop=stop)


def TR(out, in_, ident):
    return lambda e: e.transpose(out, in_, ident)


def ACT(out, in_, func, bias=None, scale=None, accum_out=None):
    kw = {}
    if bias is not None:
        kw['bias'] = bias
    if scale is not None:
        kw['scale'] = scale
    if accum_out is not None:
        kw['accum_out'] = accum_out
    return lambda e: e.activation(out=out, in_=in_, func=func, **kw)


def TT(out, in0, in1, op):
    return lambda e: e.tensor_tensor(out=out, in0=in0, in1=in1, op=op)


def TS(out, in0, s1, s2=None, op0=ALU.mult, op1=None):
    if op1 is None:
        return lambda e: e.tensor_scalar(out=out, in0=in0, scalar1=s1, scalar2=None, op0=op0)
    return lambda e: e.tensor_scalar(out=out, in0=in0, scalar1=s1, scalar2=s2, op0=op0, op1=op1)


def STT(out, in0, scalar, in1, op0, op1):
    return lambda e: e.scalar_tensor_tensor(out=out, in0=in0, scalar=scalar, in1=in1, op0=op0, op1=op1)


def CP(eng, out, in_):
    if eng == 'act':
        return lambda e: e.copy(out=out, in_=in_)
    return lambda e: e.tensor_copy(out=out, in_=in_)


def MS(ap, val):
    return lambda e: e.memset(ap, val)


def RC(out, in_):
    return lambda e: e.reciprocal(out=out, in_=in_)


def DMA(out, in_):
    return lambda e: e.dma_start(out=out, in_=in_)


class Arena:
    def __init__(self, base_ap, words):
        self.base = base_ap
        self.words = words
        self.off = 0

    def alloc(self, shape, dt):
        n = 1
        for s in shape:
            n *= s
        esz = 4 if dt == F32 else 2
        w = (n * esz + 3) // 4
        w = (w + 7) // 8 * 8
        assert self.off + w <= self.words, ("arena overflow", self.off, w, self.words)
        ap = self.base[:, self.off:self.off + w]
        self.off += w
        if dt != F32:
            ap = ap.bitcast(dt)
        ap = ap[:, 0:n]
        if len(shape) == 2:
            ap = ap.rearrange("p (a b) -> p a b", b=shape[1])
        elif len(shape) == 3:
            ap = ap.rearrange("p (a b c) -> p a b c", b=shape[1], c=shape[2])
        elif len(shape) == 4:
            ap = ap.rearrange("p (a b c d) -> p a b c d", b=shape[1], c=shape[2], d=shape[3])
        return ap


def build(TP=8192, debug=False, stage=7):
    NSUP = TP // 256
    NT1 = TP + 256
    CK = min(1024, TQ0 := TP // 4)
    SPC = CK // 256
    NCK = TP // CK
    CPQ = (TP // 4) // CK
    TQ = TP // 4
    N3 = 2 + TQ + 64
    NKT = TP // 128

    nc = bass.Bass("TRN2", target_bir_lowering=False)

    def din(name, shape, dt=F32):
        return nc.dram_tensor(name, list(shape), dt, kind="ExternalInput").ap()

    def dout(name, shape, dt=F32):
        return nc.dram_tensor(name, list(shape), dt, kind="ExternalOutput").ap()

    xT1 = din("xT1", [D, NT1])
    w_in1 = din("w_in1", [D, WIN])
    w_a2g = din("w_a2g", [16, 128])
    b_ag = din("b_ag", [128, 1])
    gng_bc_d = din("gng_bc", [128, 256])
    dng_bc_d = din("dng_bc", [128, 256])
    lamv_d = din("lamv", [128, 4, 64])
    ckT = din("ckT", [16, 2, 128, 1024])
    cv = din("cv", [16, 2, 128, 8, 128])
    s0_d = din("s0", [16, 128, 256])
    ident_d = din("ident", [128, 128])
    cm_d = din("cm", [128, 128])
    smk_d = din("smk", [128, 128])
    rmask_d = din("rmask", [128, 2, 256])
    bmask_d = din("bmask", [128, 8, 2, 16])
    x3 = din("x3", [N3, D])
    w_out_d = din("w_out", [D, D])
    w_up_d = din("w_up", [D, 2 * DFF])
    w_down_d = din("w_down", [DFF, D])
    lnbc_d = din("lnbc", [128, 4, D])
    convp_d = din("convp", [128, NCH, 4])
    cfc_d = din("cfc", [128, NCH, 4, 2])
    flag_d = din("flag", [128, 1])
    o_kv = dout("o_kv", [NT1, 512])
    o_S = dout("o_S", [17, 128, 256])
    y3 = dout("y3", [TQ + 64, D])
    o_fc = dout("o_fc", [128, NCH * 5 * 2])
    if debug:
        o_mixT = dout("o_mixT", [NCK + 1, 256, CK], BF16)
    mixedT = nc.dram_tensor("mixedT", [NCK + 1, 256, CK], BF16)
    gath = nc.dram_tensor("gath", [NCK + 2, 4 * 256, CK], BF16)
    wdn_bf = nc.dram_tensor("wdn_bf", [DFF, D], BF16)
    wup_bf = nc.dram_tensor("wup_bf", [D, 2 * DFF], BF16)
    wout_bf = nc.dram_tensor("wout_bf", [D, D], BF16)

    with ExitStack() as es:
        AW = 52800
        arena_t = es.enter_context(nc.sbuf_tensor("arena", [128, AW], F32))
        ps = es.enter_context(nc.psum_tensor("ps", [128, 8, 512], F32))
        ar = Arena(arena_t, AW)
        P = Prog(nc, es)

        ident = ar.alloc([128], BF16)
        zero2 = ar.alloc([8, 2], BF16)
        persist_off = ar.off

        w_in_sb = ar.alloc([KC, WIN], BF16)
        kdT = ar.alloc([2, TP], BF16)
        V_sb = ar.alloc([NKT, 2, 130], BF16)
        xb = [ar.alloc([KC, 256], BF16) for _ in range(2)]
        xs32 = ar.alloc([KC, 256], F32)
        A_bf = ar.alloc([2, 512], BF16)
        B_f = [ar.alloc([2, 512], F32) for _ in range(2)]
        dk_bf = ar.alloc([2, 256], BF16)
        v_bf = ar.alloc([2, 256], BF16)
        gg_f = ar.alloc([2, 256], F32)
        egg = ar.alloc([2, 256], F32)
        sigD2 = [ar.alloc([2, 512], F32) for _ in range(2)]
        glow_bf = ar.alloc([2, 128], BF16)
        qkT = ar.alloc([2, 256], BF16)
        qdT = [ar.alloc([2, 256], BF16) for _ in range(2)]
        glowT = ar.alloc([256], BF16)
        kdTs = ar.alloc([2, 256], BF16)
        Vs_sb = ar.alloc([2, 2, 130], BF16)
        w_a2_sb = ar.alloc([128], BF16)
        nb_a = ar.alloc([1], F32)
        Lb = ar.alloc([256], F32)
        cumL = ar.alloc([256], F32)
        eq = ar.alloc([256], F32)
        ek = ar.alloc([256], F32)
        eend = ar.alloc([256], F32)
        q_decT = ar.alloc([256], BF16)
        k_invT = ar.alloc([256], BF16)
        k_endT = ar.alloc([256], BF16)
        attT_bf = ar.alloc([128], BF16)
        k_end_bf = ar.alloc([128], BF16)
        S_f = ar.alloc([256], F32)
        S_bf = ar.alloc([256], BF16)
        S0_f = [ar.alloc([256], F32) for _ in range(2)]
        S0_bf = [ar.alloc([256], BF16) for _ in range(2)]
        Sn_f = [ar.alloc([256], F32) for _ in range(2)]
        qdm = ar.alloc([8, 128], BF16)
        kem = ar.alloc([8, 128], BF16)
        kend_m = ar.alloc([8, 128], BF16)
        mixa2 = [ar.alloc([2, 256], F32) for _ in range(2)]
        t1 = ar.alloc([256], F32)
        t2 = ar.alloc([256], F32)
        t3 = ar.alloc([256], F32)
        st4 = ar.alloc([16], F32)
        PT = [ar.alloc([2, 256], BF16) for _ in range(4)]
        O_sb = ar.alloc([4, 129], F32)
        rden = ar.alloc([8], F32)
        od = ar.alloc([2, 128], F32)
        od2 = ar.alloc([2, 128], F32)
        ssq = ar.alloc([8], F32)
        mix_f = ar.alloc([2, 256], F32)
        mix_bf = ar.alloc([2, 256], BF16)
        mixT = [ar.alloc([2, 256], BF16) for _ in range(2)]
        cm = ar.alloc([128], F32)
        smk = ar.alloc([128], F32)
        rmask = ar.alloc([2, 256], F32)
        gng_bc = ar.alloc([256], F32)
        dng_bc = ar.alloc([256], F32)
        bmask = ar.alloc([8, 2, 16], F32)
        lamv = ar.alloc([4, 64], F32)
        lamt = ar.alloc([2, 64], F32)
        lams = ar.alloc([8], F32)
        nlam = ar.alloc([1], F32)
        p1_end = ar.off
        KcT = [kdT[:, 0, 0:1024], kdT[:, 0, 1024:2048]] if TP >= 8192 else None
        if KcT is None:
            KcT = [ar.alloc([1024], BF16) for _ in range(2)]
            Vc = [ar.alloc([8, 130], BF16) for _ in range(2)]
            PTs = ar.alloc([2, 9, 128], BF16)
            Kst = [ar.alloc([1024], F32) for _ in range(2)]
            Vst = [ar.alloc([8, 128], F32) for _ in range(2)]
        else:
            st32 = kdT[:, 1, :].bitcast(F32)
            Kst = [st32[:, 0:1024], st32[:, 1024:2048]]
            Vst = [st32[:, 2048:3072].rearrange("p (a b) -> p a b", b=128),
                   st32[:, 3072:4096].rearrange("p (a b) -> p a b", b=128)]
            Vc = [kdT[:, 0, 2048:2048 + 1040].rearrange("p (a b) -> p a b", b=130),
                  kdT[:, 0, 3200:3200 + 1040].rearrange("p (a b) -> p a b", b=130)]
            PTs = kdT[:, 0, 4352:4352 + 2304].rearrange("p (a b c) -> p a b c", b=9, c=128)
        p1_total = ar.off

        def pSbuf(x):
            return ps[:, 2 * x:2 * x + 2, :]

        def pSk(x):
            return [('ps', 2 * x), ('ps', 2 * x + 1)]

        def pOacc(k):
            return ps[:, 4 + k // 3, (k % 3) * 129:(k % 3) * 129 + 129]

        def pOk(k):
            return ('ps', 4 + k // 3)
        M0 = ps[:, 6, :]
        M1 = ps[:, 7, :]
        M0b = ps[:, 6, :].bitcast(BF16)
        M1b = ps[:, 7, :].bitcast(BF16)
        K6 = ('ps', 6)
        K7 = ('ps', 7)

        w_in1v = w_in1.rearrange("(kc p) n -> p kc n", p=128)
        for g_, (c0_, cw_) in enumerate([(0, 512), (512, 512), (1024, 512), (1536, 512), (2048, 16)]):
            P.dma('pool', DMA(w_in_sb[:, :, c0_:c0_ + cw_], w_in1v[:, :, c0_:c0_ + cw_]), 'wld', w=[('w_in', g_)])
        xT1v = xT1.rearrange("(kc p) t -> p kc t", p=128)

        def load_x(T):
            slot = T % 2
            P.dma('sp', DMA(xs32[:, :, :], xT1v[:, :, T * 256:(T + 1) * 256]), 'xld', w=['xs32'], nlanes=2)
            P.op('act', CP('act', xb[slot][:, 0:4, :], xs32[:, 0:4, :]), r=['xs32'], w=[('xb', slot)])
            P.op('dve', CP('dve', xb[slot][:, 4:8, :], xs32[:, 4:8, :]), r=['xs32', ('xb', slot)], w=[('xb', slot)])
        load_x(0)
        load_x(1)
        P.op('pool', MS(w_a2_sb[:, :], 0.0), w=['w_a2'])
        P.op('pool', MS(glow_bf[:, :, :], 0.0), w=['glow0'])
        P.dma('pool', DMA(w_a2_sb[0:16, :], w_a2g[:, :]), 'wld', r=['w_a2'], w=['w_a2'])
        P.dma('pool', DMA(ident[:, :], ident_d[:, :]), 'wld', w=['ident'])
        P.dma('sp', DMA(nb_a[:, :], b_ag[:, :]), 'cld', w=['nb_a'])
        P.dma('sp', DMA(cm[:, :], cm_d[:, :]), 'cld', w=['cm'])
        P.dma('sp', DMA(smk[:, :], smk_d[:, :]), 'cld', w=['smk'])
        P.dma('sp', DMA(rmask[:, :, :], rmask_d[:, :, :]), 'cld', w=['rmask'])
        P.dma('sp', DMA(gng_bc[:, :], gng_bc_d[:, :]), 'cld', w=['gng'])
        P.dma('sp', DMA(dng_bc[:, :], dng_bc_d[:, :]), 'cld', w=['dng'])
        P.dma('sp', DMA(bmask[:, :, :, :], bmask_d[:, :, :, :]), 'cld', w=['bmask'])
        P.dma('sp', DMA(lamv[:, :, :], lamv_d[:, :, :]), 'cld', w=['lamv'])
        bulk = []
        for c in range(0, NCH, 2):
            bulk.append((DMA(wdn_bf[c * 128:(c + 2) * 128, :], w_down_d[c * 128:(c + 2) * 128, :]), [('wdn', c), ('wdn', c + 1)]))
        for kc in range(KC):
            bulk.append((DMA(wup_bf[kc * 128:(kc + 1) * 128, :], w_up_d[kc * 128:(kc + 1) * 128, :]), [('wupd', kc)]))
        for kc in range(0, KC, 2):
            bulk.append((DMA(wout_bf[kc * 128:(kc + 2) * 128, :], w_out_d[kc * 128:(kc + 2) * 128, :]), [('woutd', kc)]))

        def issue_bulk(n):
            for _ in range(n):
                if bulk:
                    fn, wk = bulk.pop(0)
                    P.dma('pool', fn, 'wdc', w=wk)
        P.op('pool', TS(nb_a[:, :], nb_a[:, :], -1.0), r=['nb_a'], w=['nb_a'])
        P.op('pool', TS(dng_bc[:, :], dng_bc[:, :], 1.0 - LAM_INIT), r=['dng'], w=['dng'])
        P.op('dve', TT(lamt[:, 0, :], lamv[:, 0, :], lamv[:, 1, :], ALU.mult), r=['lamv'], w=['lamt'])
        P.op('dve', TT(lamt[:, 1, :], lamv[:, 2, :], lamv[:, 3, :], ALU.mult), r=['lamv', 'lamt'], w=['lamt'])
        P.op('dve', lambda e: e.reduce_sum(out=lams[:, 0:2], in_=lamt[:, :, :], axis=mybir.AxisListType.X),
             r=['lamt'], w=['lams'])
        P.op('act', ACT(lams[:, 2:4], lams[:, 0:2], AF.Exp), r=['lams'], w=['lams'])
        P.op('dve', TT(lams[:, 4:5], lams[:, 3:4], lams[:, 2:3], ALU.subtract), r=['lams'], w=['lams'])
        P.op('dve', TS(nlam[:, :], lams[:, 4:5], -LAM_INIT, None, ALU.add), r=['lams'], w=['nlam'])
        P.op('pool', MS(zero2[:, :, :], 0.0), w=['zero2'])
        P.op('pool', MS(V_sb[:, :, :, 128:130], 1.0), w=['Vones'])
        P.op('pool', MS(Vs_sb[:, :, :, 128:130], 1.0), w=['Vsones'])
        P.op('pool', MS(qdm[:, :, :], 0.0), w=['qdm'])
        P.op('pool', MS(kem[:, :, :], 0.0), w=['kem'])
        P.op('pool', MS(S_f[:, :], 0.0), w=['S_f'])
        P.op('pool', MS(S_bf[:, :], 0.0), w=['S_bf'])
        mz = mixedT.ap()
        gv = gath.ap()
        P.dma('sp', DMA(gv[0, :, CK - 2:CK].rearrange("(c p) t -> p c t", p=128), zero2[:, :, :]), 'cld', r=['zero2'],
              w=['gath_pad'])

        GRP = [(0, 512), (512, 512), (1024, 512), (1536, 512), (2048, 16)]
        live7 = [0]

        def prep(T):
            slot = T % 2
            sample = (T == NSUP)
            tok0 = T * 256
            bslot = T % 2
            sigD = sigD2[slot]
            mixa = mixa2[slot]
            gi = 0
            for sub in range(2):
                for g, (c0, cw) in enumerate(GRP):
                    bank = M0 if gi % 2 == 0 else M1
                    bkey = K6 if gi % 2 == 0 else K7
                    gi += 1
                    for kc in range(KC):
                        P.op('pe', MM(bank[:, 0:cw], xb[slot][:, kc, sub * 128:(sub + 1) * 128],
                                      w_in_sb[:, kc, c0:c0 + cw], start=(kc == 0), stop=(kc == KC - 1)),
                             r=[('xb', slot), ('w_in', g)], w=[bkey])
                    if bkey == K7:
                        live7[0] += 1
                    yield
                    if bkey == K7:
                        live7[0] -= 1
                    if g == 0:
                        P.op('dve', CP('dve', A_bf[:, sub, :], bank[:, 0:512]), r=[bkey], w=[('A_bf', sub)])
                    elif g == 1:
                        P.op('dve', CP('dve', B_f[bslot][:, sub, :], bank[:, 0:512]), r=[bkey], w=[('B_f', bslot, sub)])
                        P.op('dve', CP('dve', dk_bf[:, sub, :], bank[:, 0:256]), r=[bkey], w=[('dk_bf', sub)])
                        if not sample:
                            P.op('dve', CP('dve', V_sb[:, 2 * T + sub, :, 0:128],
                                           bank[:, 256:512].rearrange("p (h e) -> p h e", e=128)),
                                 r=[bkey, 'Vones'], w=[('V', 2 * T + sub)])
                        else:
                            P.op('dve', CP('dve', Vs_sb[:, sub, :, 0:128],
                                           bank[:, 256:512].rearrange("p (h e) -> p h e", e=128)),
                                 r=[bkey, 'Vsones'], w=[('Vs', sub)])
                    elif g == 2:
                        P.op('dve', CP('dve', v_bf[:, sub, :], bank[:, 0:256]), r=[bkey], w=[('v_bf', sub)])
                        P.op('dve', CP('dve', gg_f[:, sub, :], bank[:, 256:512]), r=[bkey], w=[('gg_f', sub)])
                        P.op('act', ACT(egg[:, sub, :], bank[:, 256:512], AF.Exp, scale=-1.0), r=[bkey], w=[('egg', sub)])
                        P.op('act', ACT(egg[:, sub, :], egg[:, sub, :], AF.Ln, bias=1.0), r=[('egg', sub)], w=[('egg', sub)])
                        P.op('act', ACT(egg[:, sub, :], egg[:, sub, :], AF.Exp, scale=-1.0), r=[('egg', sub)], w=[('egg', sub)])
                    elif g == 3:
                        P.op('act', ACT(sigD[:, sub, :], bank[:, 0:512], AF.Exp, scale=-1.0), r=[bkey], w=[('sigD', slot, sub)])
                        P.op('act', ACT(sigD[:, sub, :], sigD[:, sub, :], AF.Ln, bias=1.0), r=[('sigD', slot, sub)],
                             w=[('sigD', slot, sub)])
                        P.op('act', ACT(sigD[:, sub, :], sigD[:, sub, :], AF.Exp, scale=-1.0), r=[('sigD', slot, sub)],
                             w=[('sigD', slot, sub)])
                    else:
                        P.op('dve', CP('dve', glow_bf[:, sub, 0:16], bank[:, 0:16]), r=[bkey, 'glow0'], w=[('glow_bf', sub)])
                yield
                for i in range(4):
                    P.op('pe', TR(M1b[:, i * 128:(i + 1) * 128], A_bf[:, sub, i * 128:(i + 1) * 128], ident[:, :]),
                         r=[('A_bf', sub), 'ident'], w=[K7])
                for i in range(2):
                    P.op('pe', TR(M1b[:, 512 + i * 128:512 + (i + 1) * 128], dk_bf[:, sub, i * 128:(i + 1) * 128],
                                  ident[:, :]), r=[('dk_bf', sub), 'ident'], w=[K7])
                P.op('pe', TR(M1b[:, 768:896], glow_bf[:, sub, :], ident[:, :]), r=[('glow_bf', sub), 'ident'], w=[K7])
                live7[0] += 1
                yield
                live7[0] -= 1
                scol = slice(sub * 128, (sub + 1) * 128)
                P.op('dve', CP('dve', qkT[:, :, scol], M1b[:, 0:256].rearrange("p (a t) -> p a t", t=128)),
                     r=[K7], w=[('qkT', sub)])
                P.op('dve', CP('dve', qdT[slot][:, :, scol], M1b[:, 256:512].rearrange("p (a t) -> p a t", t=128)),
                     r=[K7], w=[('qdT', slot)])
                if not sample:
                    P.op('dve', CP('dve', kdT[:, :, tok0 + sub * 128:tok0 + (sub + 1) * 128],
                                   M1b[:, 512:768].rearrange("p (a t) -> p a t", t=128)),
                         r=[K7], w=[('kdT', 2 * T + sub)])
                else:
                    P.op('dve', CP('dve', kdTs[:, :, scol], M1b[:, 512:768].rearrange("p (a t) -> p a t", t=128)),
                         r=[K7], w=[('kdTs', sub)])
                P.op('dve', CP('dve', glowT[:, scol], M1b[:, 768:896]), r=[K7], w=[('glowT', sub)])
                yield
            P.dma('sp', DMA(o_kv[tok0:tok0 + 256, :].rearrange("(s p) c -> p s c", p=128), B_f[bslot][:, :, :]), 'okv',
                  r=[('B_f', bslot, 0), ('B_f', bslot, 1)])
            P.op('pe', MM(M0[:, 0:256], w_a2_sb[:, :], glowT[:, :]), r=['w_a2', ('glowT', 0), ('glowT', 1)], w=[K6])
            yield
            P.op('act', ACT(Lb[:, :], M0[:, 0:256], AF.Exp, bias=nb_a[:, 0:1], scale=-1.0), r=[K6, 'nb_a'], w=['Lb'])
            P.op('act', ACT(Lb[:, :], Lb[:, :], AF.Ln, bias=1.0), r=['Lb'], w=['Lb'])
            yield
            rm = rmask[:, 1, :] if sample else rmask[:, 0, :]
            P.op('dve', lambda e: e.tensor_tensor_scan(out=cumL[:, :], data0=rm, data1=Lb[:, :], initial=0.0,
                                                       op0=ALU.mult, op1=ALU.add), r=['Lb', 'rmask'], w=['cumL'])
            yield
            yield
            P.op('act', ACT(eq[:, :], cumL[:, :], AF.Exp, scale=-1.0 / 16.0), r=['cumL'], w=['eq'])
            P.op('act', ACT(ek[:, :], cumL[:, :], AF.Exp, scale=1.0 / 16.0), r=['cumL'], w=['ek'])
            yield
            yield
            nchunk = 16 if sample else 2
            cl = 256 // nchunk
            for ch in range(nchunk):
                P.op('dve', TS(eend[:, ch * cl:(ch + 1) * cl], ek[:, ch * cl:(ch + 1) * cl],
                               eq[:, (ch + 1) * cl - 1:(ch + 1) * cl]), r=['eq', 'ek'], w=['eend'])
            P.op('dve', STT(q_decT[:, :], qkT[:, 0, :], 128.0 ** -0.5, eq[:, :], ALU.mult, ALU.mult),
                 r=[('qkT', 0), ('qkT', 1), 'eq'], w=['q_decT'])
            P.op('dve', TT(k_invT[:, :], qkT[:, 1, :], ek[:, :], ALU.mult), r=[('qkT', 0), ('qkT', 1), 'ek'], w=['k_invT'])
            P.op('dve', TT(k_endT[:, :], qkT[:, 1, :], eend[:, :], ALU.mult), r=[('qkT', 0), ('qkT', 1), 'eend'],
                 w=['k_endT'])
            for _ in range(5):
                yield
            for sub in range(2):
                scol = slice(sub * 128, (sub + 1) * 128)
                P.op('pe', MM(M0[:, 256:384], k_invT[:, scol], q_decT[:, scol]), r=['k_invT', 'q_decT'], w=[K6])
                yield
                P.op('dve', TT(attT_bf[:, :], M0[:, 256:384], smk[:, :] if sample else cm[:, :], ALU.mult),
                     r=[K6, 'cm', 'smk'], w=['attT'])
                yield
                if not sample:
                    P.op('pe', TR(M0b[:, 768:896], k_endT[:, scol], ident[:, :]), r=['k_endT', 'ident'], w=[K6])
                    yield
                    P.op('dve', CP('dve', k_end_bf[:, :], M0b[:, 768:896]), r=[K6], w=['k_end'])
                    yield
                    live7[0] += 1
                    P.op('pe', MM(M1[:, 0:256], attT_bf[:, :], v_bf[:, sub, :], start=True, stop=False),
                         r=['attT', ('v_bf', sub)], w=[K7])
                    P.op('pe', MM(M1[:, 0:256], q_decT[:, scol], S_bf[:, :], start=False, stop=True),
                         r=['q_decT', 'S_bf'], w=[K7])
                    P.op('pe', MM(M1[:, 256:512], k_end_bf[:, :], v_bf[:, sub, :]), r=['k_end', ('v_bf', sub)], w=[K7])
                    yield
                    P.op('dve', STT(S_f[:, :], S_f[:, :], eq[:, sub * 128 + 127:sub * 128 + 128], M1[:, 256:512],
                                    ALU.mult, ALU.add), r=['S_f', 'eq', K7], w=['S_f'])
                    P.op('act', CP('act', S_bf[:, :], S_f[:, :]), r=['S_f'], w=['S_bf'])
                    if T == NSUP - 1 and sub == 1:
                        P.dma('sp', DMA(o_S[0, :, :], S_f[:, :]), 'oS', r=['S_f'])
                else:
                    for j in range(8):
                        c16 = slice(sub * 128 + j * 16, sub * 128 + (j + 1) * 16)
                        P.op('pool', CP('pool', qdm[:, j, j * 16:(j + 1) * 16], q_decT[:, c16]), r=['q_decT'], w=['qdm'])
                        P.op('pool', CP('pool', kem[:, j, j * 16:(j + 1) * 16], k_endT[:, c16]), r=['k_endT'], w=['kem'])
                    for j in range(8):
                        P.op('pe', TR(M0b[:, j * 128:(j + 1) * 128], kem[:, j, :], ident[:, :]), r=['kem', 'ident'], w=[K6])
                    P.op('act', CP('act', kend_m[:, :, :], M0b[:, :].rearrange("p (j d) -> p j d", d=128)),
                         r=[K6], w=['kend_m'])
                    live7[0] += 1
                    P.op('pe', MM(M1[:, 0:256], attT_bf[:, :], v_bf[:, sub, :], start=True, stop=False),
                         r=['attT', ('v_bf', sub)], w=[K7])
                    for j in range(8):
                        bb = sub * 8 + j
                        ss = bb % 2
                        P.dma('sp', DMA(S0_f[ss][:, :], s0_d[bb, :, :]), 's0', w=[('S0_f', ss)], nlanes=2)
                        P.op('pool', CP('pool', S0_bf[ss][:, :], S0_f[ss][:, :]), r=[('S0_f', ss)], w=[('S0_bf', ss)])
                        P.op('pe', MM(M1[:, 0:256], qdm[:, j, :], S0_bf[ss][:, :], start=False, stop=(j == 7)),
                             r=['qdm', ('S0_bf', ss)], w=[K7])
                        P.op('pe', MM(M0[:, 0:256], kend_m[:, j, :], v_bf[:, sub, :]), r=['kend_m', ('v_bf', sub)], w=[K6])
                        P.op('dve', STT(Sn_f[ss][:, :], S0_f[ss][:, :],
                                        eq[:, sub * 128 + j * 16 + 15:sub * 128 + j * 16 + 16], M0[:, 0:256],
                                        ALU.mult, ALU.add), r=[('S0_f', ss), 'eq', K6], w=[('Sn_f', ss)])
                        P.dma('sp', DMA(o_S[1 + bb, :, :], Sn_f[ss][:, :]), 'oS', r=[('Sn_f', ss)])
                    for j in range(8):
                        P.op('pool', MS(qdm[:, j, j * 16:(j + 1) * 16], 0.0), r=[], w=['qdm'])
                        P.op('pool', MS(kem[:, j, j * 16:(j + 1) * 16], 0.0), r=[], w=['kem'])
                yield
                P.op('act', ACT(t1[:, :], M1[:, 0:256], AF.Square, accum_out=st4[:, 0:1]), r=[K7], w=['t1', 'st4'])
                P.op('act', ACT(st4[:, 1:2], st4[:, 0:1], AF.Ln, bias=EPS, scale=1.0 / 256.0), r=['st4'], w=['st4'])
                P.op('act', ACT(st4[:, 2:3], st4[:, 1:2], AF.Exp, scale=-0.5), r=['st4'], w=['st4'])
                yield
                yield
                P.op('dve', STT(t2[:, :], M1[:, 0:256], st4[:, 2:3], gng_bc[:, :], ALU.mult, ALU.mult),
                     r=[K7, 'st4', 'gng'], w=['t2'])
                live7[0] -= 1
                P.op('dve', TT(t3[:, :], egg[:, sub, :], gg_f[:, sub, :], ALU.mult), r=[('egg', sub), ('gg_f', sub)], w=['t3'])
                P.op('dve', TT(t2[:, :], t2[:, :], t3[:, :], ALU.mult), r=['t2', 't3'], w=['t2'])
                P.op('dve', TT(mixa[:, sub, :], t2[:, :], sigD[:, sub, 0:256], ALU.mult), r=['t2', ('sigD', slot, sub)],
                     w=[('mixa', slot, sub)])
                yield

        deferred = []

        def flush_deferred(upto=10 ** 9, filler=None):
            last = upto >= 10 ** 9
            while deferred:
                d, fn, needs7 = deferred[0]
                if d > upto:
                    break
                if needs7 and live7[0] > 0:
                    if not last:
                        break
                    while live7[0] > 0:
                        fill(filler, 1)
                deferred.pop(0)
                fn()

        def post_q(T, qs):
            mslot = T % 2
            sigD = sigD2[mslot]
            mixa = mixa2[mslot]
            k4 = ('ps', 4)
            k5 = ('ps', 5)
            P.op('dve', CP('dve', O_sb[:, 0:3, :], ps[:, 4, 0:387].rearrange("p (a c) -> p a c", c=129)), r=[k4], w=['O_sb'])
            P.op('dve', CP('dve', O_sb[:, 3, :], ps[:, 5, 0:129]), r=[k5, 'O_sb'], w=['O_sb'])

            def st_a():
                P.op('dve', RC(rden[:, 0:4], O_sb[:, 0:4, 128]), r=['O_sb'], w=['rden'])
                for h in range(2):
                    k1 = h * 2
                    k2 = h * 2 + 1
                    P.op('dve', TS(od2[:, h, :], O_sb[:, k2, 0:128], rden[:, k2:k2 + 1], nlam[:, 0:1], ALU.mult, ALU.mult),
                         r=['O_sb', 'rden', 'nlam'], w=[('od2', h)])
                    P.op('dve', STT(od[:, h, :], O_sb[:, k1, 0:128], rden[:, k1:k1 + 1], od2[:, h, :], ALU.mult, ALU.add),
                         r=['O_sb', 'rden', ('od2', h)], w=[('od', h)])

            def st_b():
                for h in range(2):
                    P.op('act', ACT(od2[:, h, :], od[:, h, :], AF.Square, accum_out=ssq[:, h:h + 1]), r=[('od', h)],
                         w=[('od2', h), ('ssq', h)])
                P.op('act', ACT(ssq[:, 2:4], ssq[:, 0:2], AF.Ln, bias=EPS, scale=1.0 / 128.0), r=[('ssq', 0), ('ssq', 1)],
                     w=['ssq2'])
                P.op('act', ACT(ssq[:, 2:4], ssq[:, 2:4], AF.Exp, scale=-0.5), r=['ssq2'], w=['ssq2'])

            def st_c():
                for h in range(2):
                    P.op('dve', STT(mix_f[:, qs, h * 128:(h + 1) * 128], od[:, h, :], ssq[:, 2 + h:3 + h],
                                    dng_bc[:, h * 128:(h + 1) * 128], ALU.mult, ALU.mult), r=[('od', h), 'ssq2', 'dng'],
                         w=[('mix_f', qs)])
                P.op('dve', TT(mix_f[:, qs, :], mix_f[:, qs, :], sigD[:, qs, 256:512], ALU.mult),
                     r=[('mix_f', qs), ('sigD', mslot, qs)], w=[('mix_f', qs)])
                P.op('dve', TT(mix_bf[:, qs, :], mix_f[:, qs, :], mixa[:, qs, :], ALU.add),
                     r=[('mix_f', qs), ('mixa', mslot, qs)], w=[('mix_bf', qs)])
            deferred.append((0, st_a, False))
            deferred.append((2, st_b, False))
            deferred.append((4, st_c, False))
            deferred.append((6, lambda: post_q_b(T, qs), True))

        def post_q_b(T, qs):
            mslot = T % 2
            for c in range(2):
                P.op('pe', TR(M1b[:, c * 128:(c + 1) * 128], mix_bf[:, qs, c * 128:(c + 1) * 128], ident[:, :]),
                     r=[('mix_bf', qs), 'ident'], w=[K7])
            P.op('act', CP('act', mixT[mslot][:, :, qs * 128:(qs + 1) * 128], M1b[:, 0:256].rearrange("p (c t) -> p c t", t=128)),
                 r=[K7], w=[('mixT', mslot)])
            if qs == 1:
                post_final(T)

        def post_final(T):
            mslot = T % 2
            ck = T // SPC
            c0 = (T % SPC) * 256
            P.dma('sp', DMA(mz[ck, :, c0:c0 + 256].rearrange("(c p) t -> p c t", p=128), mixT[mslot][:, :, :]),
                  'omix', r=[('mixT', mslot)], w=[('mixedT', T)])
            if T == NSUP or (T % SPC) == SPC - 1:
                cc_pending.append(ck)
            if CCP and T >= T_CC and T < NSUP:
                if not cc_ops:
                    issue_bulk(100)
                    extra = [o for ln, o in P.lane_last.items() if ln[0] == 'wdc']
                else:
                    extra = []
                for _ in range(2):
                    if not cc_pending:
                        break
                    k_ = cc_pending.pop(0)
                    cc_ops.append(P.dma('pool', mk_cc(k_), 'cc', deps=extra,
                                        r=[('mixedT', t) for t in range(k_ * SPC, min((k_ + 1) * SPC, NSUP + 1))],
                                        w=[('gath', k_)], nlanes=1, inc=1))

        cc_ops = []
        cc_pending = []
        CCP = os.environ.get('KCCP', '1') == '1'
        T_CC = min(27, max(1, NSUP - 3))

        def mk_cc(ck):
            return lambda e: e.collective_compute("AllGather", ALU.bypass, replica_groups=[[0, 1, 2, 3], [4, 5, 6, 7]],
                                                  ins=[mz[ck, :, :]], outs=[gv[ck + 1, :, :]])

        def fill(gen, n=1):
            if gen is None:
                return
            for _ in range(n):
                try:
                    next(gen)
                except StopIteration:
                    return

        def drain(gen):
            if gen is None:
                return
            for _ in gen:
                pass

        kctr = [0]

        def attention_prompt(T, qs, filler):
            slot = T % 2
            qi = 2 * T + qs
            qc = slice(qs * 128, (qs + 1) * 128)
            k0 = kctr[0]
            kctr[0] += qi + 1

            def rec_S(j):
                x = (k0 + j) % 2
                for h in range(2):
                    for m in range(2):
                        P.op('pe', MM(pSbuf(x)[:, m, h * 128:(h + 1) * 128], kdT[64 * m:64 * m + 64, h, j * 128:(j + 1) * 128],
                                      qdT[slot][64 * m:64 * m + 64, h, qc]),
                             r=[('kdT', j), ('qdT', slot)], w=pSk(x))

            rec_S(0)
            for j in range(qi + 1):
                if j + 1 <= qi:
                    rec_S(j + 1)
                k = k0 + j
                x = k % 2
                pt = PT[k % 4]
                pk = ('PT', k % 4)
                P.op('act', ACT(pt[:, :, :], pSbuf(x)[:, :, 0:256], AF.Exp, scale=0.125), r=pSk(x), w=[pk])
                if j == qi:
                    P.op('dve', MS(pt[64:128, :, :].rearrange("p m (h q) -> p m h q", q=128)[:, :, :, 0:64], 0.0), r=[], w=[pk])
                fill(filler, 1)
                flush_deferred(j if j < qi else 10 ** 9, filler)
                for h in range(2):
                    for m in range(2):
                        kacc = h * 2 + m
                        P.op('pe', MM(pOacc(kacc), pt[:, m, h * 128:(h + 1) * 128], V_sb[:, j, h, 0:129],
                                      start=(j == 0 and kacc in (0, 3)), stop=(j == qi), skip=True),
                             r=[pk, ('V', j)], w=[pOk(kacc)])

        def attention_sample(sub):
            T = NSUP
            slot = T % 2
            for jj in range(8):
                bb = sub * 8 + jj
                for h in range(2):
                    it = bb * 2 + h
                    sl = it % 2
                    x = it % 2
                    P.dma('sp', DMA(Kst[sl], ckT[bb, h, :, :]), 'kc', w=[('Kst', sl)], nlanes=2)
                    P.dma('sp', DMA(Vst[sl], cv[bb, h, :, :, :]), 'vc', w=[('Vst', sl)], nlanes=2)
                    P.op('act', CP('act', KcT[sl], Kst[sl]), r=[('Kst', sl)], w=[('KcT', sl)])
                    P.op('dve', CP('dve', Vc[sl][:, :, 0:128], Vst[sl]), r=[('Vst', sl), ('Vc1', sl)], w=[('Vc', sl)])
                    qc = slice(bb * 16, bb * 16 + 16)
                    for kt in range(9):
                        for m in range(2):
                            if kt < 8:
                                lhs = KcT[sl][64 * m:64 * m + 64, kt * 128:(kt + 1) * 128]
                                rk = [('KcT', sl)]
                            else:
                                lhs = kdTs[64 * m:64 * m + 64, h, sub * 128:(sub + 1) * 128]
                                rk = [('kdTs', sub)]
                            P.op('pe', MM(pSbuf(x)[:, m, kt * 16:(kt + 1) * 16], lhs, qdT[slot][64 * m:64 * m + 64, h, qc]),
                                 r=rk + [('qdT', slot)], w=pSk(x))
                    P.op('act', ACT(PTs[:, :, :, jj * 16:(jj + 1) * 16],
                                    pSbuf(x)[:, :, 0:144].rearrange("p m (k q) -> p m k q", q=16), AF.Exp, scale=0.125),
                         r=pSk(x), w=['PTs'])
                    P.op('dve', TT(PTs[:, :, 8, jj * 16:(jj + 1) * 16], PTs[:, :, 8, jj * 16:(jj + 1) * 16],
                                   bmask[:, jj, :, :], ALU.mult), r=['PTs', 'bmask'], w=['PTs'])
                    for kt in range(9):
                        for m in range(2):
                            kacc = h * 2 + m
                            rhs = Vc[sl][:, kt, 0:129] if kt < 8 else Vs_sb[:, sub, h, 0:129]
                            rk = [('Vc', sl)] if kt < 8 else [('Vs', sub)]
                            P.op('pe', MM(pOacc(kacc), PTs[:, m, kt, :], rhs,
                                          start=(jj == 0 and kt == 0 and kacc in (0, 3)),
                                          stop=(jj == 7 and kt == 8), skip=True),
                                 r=['PTs'] + rk, w=[pOk(kacc)])
                    P.op('pool', MS(PTs[:, :, :, jj * 16:(jj + 1) * 16], 0.0), r=[], w=['PTs'])

        def done():
            P.emit()
            return nc, dict(n_sems=P.n_sems, n_ops=P.n_ops)

        if stage == 1:
            issue_bulk(100)
            return done()
        if stage == 2:
            fill(prep(0), int(os.environ.get("KSTEPS", "100")))
            return done()
        drain(prep(0))
        if stage == 3:
            for qs in range(2):
                attention_prompt(0, qs, None)
                post_q(0, qs)
            flush_deferred()
            return done()
        for T in range(NSUP):
            nxt = prep(T + 1)
            for qs in range(2):
                attention_prompt(T, qs, nxt)
                post_q(T, qs)
            drain(nxt)
            if T + 2 <= NSUP:
                load_x(T + 2)
            if T >= 3:
                issue_bulk(1)
        issue_bulk(100)
        flush_deferred()
        if stage == 4:
            return done()
        P.barrier()
        P.op('pool', MS(PTs[:, :, :, :], 0.0), w=['PTs'])
        for sl in range(2):
            P.op('pool', MS(Vc[sl][:, :, 128:130], 1.0), w=[('Vc1', sl)])
        for sub in range(2):
            attention_sample(sub)
            post_q(NSUP, sub)
            flush_deferred()

        if debug:
            bd = P.barrier()
            P.dma('sp', DMA(o_mixT[:, :, :], mz[:, :, :]), 'dbg', deps=bd)
        if stage == 5:
            return done()

        bar0 = P.barrier()
        ar.off = persist_off
        w_up_sb = ar.alloc([KC, 2 * DFF], BF16)
        w_out_sb = ar.alloc([KC, D], BF16)
        wl_ops = {}
        for kc in range(KC):
            wl_ops[('w_out', kc)] = P.dma('sp', DMA(w_out_sb[:, kc, :], wout_bf[kc * 128:(kc + 1) * 128, :]), 'wl3',
                                          w=[('w_out', kc)], deps=bar0)
        wupv = wup_bf.ap().rearrange("(kc p) n -> p kc n", p=128)
        for g6 in range(4):
            c0 = g6 * 768
            cw = min(768, DFF - c0)
            for part in range(2):
                wl_ops[('w_up', g6, part)] = P.dma('sp', DMA(w_up_sb[:, :, part * DFF + c0:part * DFF + c0 + cw],
                                                             wupv[:, :, part * DFF + c0:part * DFF + c0 + cw]),
                                                   'wl3', w=[('w_up', g6, part)], deps=bar0)
        for ck in cc_pending:
            cc_ops.append(P.dma('pool', mk_cc(ck), 'cc', deps=bar0, w=[('gath', ck)], nlanes=1, inc=1))
        cc = cc_ops[-1]
        bar = P.barrier(exclude=('wl3',))
        for k_, o_ in wl_ops.items():
            P.last_w[k_] = o_

        if stage == 6:
            return done()
        wd_sb = [ar.alloc([D], BF16) for _ in range(4)]
        lnbc = ar.alloc([4, D], F32)
        x_sb = ar.alloc([2, D], F32)
        zs = ar.alloc([2, D], F32)
        x1_f = ar.alloc([2, D], F32)
        zs2 = ar.alloc([2, D], F32)
        x1_bf = ar.alloc([2, D], BF16)
        mixT_sb = [ar.alloc([KC, 256], BF16) for _ in range(2)]
        x1T_sb = ar.alloc([KC, 256], BF16)
        a_ext = [ar.alloc([258], F32) for _ in range(4)]
        cacc = [ar.alloc([256], F32) for _ in range(4)]
        gl = [ar.alloc([256], F32) for _ in range(4)]
        h_bf = [ar.alloc([256], BF16) for _ in range(4)]
        halo_all = ar.alloc([NCH, 2], F32)
        fc_sb = ar.alloc([NCH, 5, 2], F32)
        convp = ar.alloc([NCH, 4], F32)
        cfc = ar.alloc([NCH, 4, 2], F32)
        flag = ar.alloc([1], F32)
        bst = ar.alloc([2, 12], F32)
        mv = ar.alloc([2, 4], F32)
        p3_total = ar.off

        def pZ(sub):
            return ps[:, 2 * sub:2 * sub + 2, :] if sub == 0 else ps[:, 4:6, :]

        def pZkeys(sub):
            return [('ps', 0), ('ps', 1)] if sub == 0 else [('ps', 4), ('ps', 5)]

        def pD(sub):
            return ps[:, 2 + 2 * sub:4 + 2 * sub, :]

        def pDkeys(sub):
            return [('ps', 2 + 2 * sub), ('ps', 3 + 2 * sub)]

        def pU(i):
            return ps[:, 6 + i, :]

        P.dma('sp', DMA(lnbc[:, :, :], lnbc_d[:, :, :]), 'cld', w=['lnbc'], deps=bar)
        P.dma('sp', DMA(convp[:, :, :], convp_d[:, :, :]), 'cld', w=['convp'], deps=bar)
        P.dma('sp', DMA(cfc[:, :, :, :], cfc_d[:, :, :, :]), 'cld', w=['cfc'], deps=bar)
        P.dma('sp', DMA(flag[:, :], flag_d[:, :]), 'cld', w=['flag'], deps=bar)

        gvv = gv.rearrange("c (kc p) t -> p (c kc) t", p=128)
        wdv = wdn_bf.ap()
        tile_ctr = [0]
        jreg = {}
        chunk_ctr = [0]

        def ffn_tile(kind, row0, n, nb, L, tix, last_main=False):
            ti = tile_ctr[0]
            tile_ctr[0] += 1
            ms_ = min(128, n)
            nsub = (n + 127) // 128
            mslot = ti % 2

            def ld_mix(e):
                if 'j' not in jreg:
                    jreg['j'] = e.snap(e.partition_id() % 4, min_val=0, max_val=3)
                j = jreg['j']
                if kind == 'halo':
                    src = gvv[:, bass.ds(j * (CPQ * 8), 8), CK - 2:CK]
                elif kind == 'main':
                    cko = 1 + (tix * 256) // CK
                    c0 = (tix * 256) % CK
                    src = gvv[:, bass.ds(j * (CPQ * 8) + cko * 8, 8), c0:c0 + 256]
                else:
                    src = gvv[:, (NCK + 1) * 8:(NCK + 2) * 8, bass.ds(j * 64, 64)]
                return e.dma_start(out=mixT_sb[mslot][:, :, 0:n], in_=src)
            P.dma('sp', ld_mix, 'mld', w=[('mixT_sb', mslot)], deps=[cc], nlanes=2)
            for sub in range(nsub):
                P.dma('sp', DMA(x_sb[0:ms_, sub, :], x3[row0 + sub * 128:row0 + sub * 128 + ms_, :]), 'xld3',
                      w=[('x_sb', sub)], nlanes=2)
            for sub in range(nsub):
                z = pZ(sub)
                for half in range(2):
                    for kc in range(KC):
                        P.op('pe', MM(z[0:ms_, half, :], mixT_sb[mslot][:, kc, sub * 128:sub * 128 + ms_],
                                      w_out_sb[:, kc, half * 512:(half + 1) * 512], start=(kc == 0), stop=(kc == KC - 1)),
                             r=[('mixT_sb', mslot), ('w_out', kc)], w=pZkeys(sub))
                P.op('dve', STT(zs[0:ms_, sub, :], x_sb[0:ms_, sub, :], ALPHA, z[0:ms_, :, :].rearrange("p a b -> p (a b)"),
                                ALU.mult, ALU.add), r=[('x_sb', sub)] + pZkeys(sub), w=[('zs', sub)])
                layer_norm(sub, ms_, zs, 0, x1_f, ('x1_f', sub))
                P.op('act', CP('act', x1_bf[0:ms_, sub, :], x1_f[0:ms_, sub, :]), r=[('x1_f', sub)], w=[('x1_bf', sub)])
                ub = ps[:, 6 + sub, :].bitcast(BF16)
                for kc in range(KC):
                    P.op('pe', TR(ub[:, kc * 128:kc * 128 + ms_], x1_bf[0:ms_, sub, kc * 128:(kc + 1) * 128],
                                  ident[0:ms_, 0:ms_]), r=[('x1_bf', sub), 'ident'], w=[('ps', 6 + sub)])
                P.op('dve', CP('dve', x1T_sb[:, :, sub * 128:sub * 128 + ms_],
                               ub[:, :].rearrange("p (k t) -> p k t", t=128)[:, :, 0:ms_]),
                     r=[('ps', 6 + sub)], w=[('x1T', sub)])
            x1Tk = [('x1T', s) for s in range(nsub)]
            UB = [0, 1, 6, 7] if os.environ.get('KUB', '4') == '4' else [6, 7, 6, 7]
            LAG = int(os.environ.get('KLAG', '2'))
            base = chunk_ctr[0]
            chunk_ctr[0] += NCH

            def rec_up(c):
                ci = base + c
                u = ps[:, UB[ci % 4], :]
                ukey = ('ps', UB[ci % 4])
                for part in range(2):
                    for kc in range(KC):
                        P.op('pe', MM(u[:, part * 256:part * 256 + n],
                                      w_up_sb[:, kc, part * DFF + c * 128:part * DFF + (c + 1) * 128],
                                      x1T_sb[:, kc, 0:n], start=(kc == 0), stop=(kc == KC - 1)),
                             r=x1Tk + [('w_up', c // 6, part)], w=[ukey])
                if kind == 'halo':
                    P.op('dve', TS(halo_all[:, c, :], u[:, 0:2], flag[:, 0:1]), r=[ukey, 'flag'], w=[('halo', c)])
                    return
                ws = ci % 4
                P.dma('sp', DMA(wd_sb[ws][:, :], wdv[c * 128:(c + 1) * 128, :]), 'wdl', r=[('wdn', c)], w=[('wd_sb', ws)])
                es_ = ci % 4
                ae = a_ext[es_][:, 0:nb * (L + 2)].rearrange("p (b l) -> p b l", l=L + 2)
                aek = ('a_ext', es_)
                P.op('act', CP('act', ae[:, :, 2:L + 2], u[:, 0:n].rearrange("p (b l) -> p b l", l=L)), r=[ukey], w=[aek])
                if kind == 'main':
                    P.op('pool', CP('pool', ae[:, 0, 0:2], halo_all[:, c, :]), r=[('halo', c)], w=[aek])
                else:
                    P.op('pool', CP('pool', ae[:, :, 0:2], cfc[:, c, :, :]), r=['cfc'], w=[aek])
                ca = cacc[es_][:, 0:n].rearrange("p (b l) -> p b l", l=L)
                cak = ('cacc', es_)
                P.op('dve', TS(ca, ae[:, :, 0:L], convp[:, c, 0:1], convp[:, c, 3:4], ALU.mult, ALU.add),
                     r=[aek, 'convp'], w=[cak])
                P.op('dve', STT(ca, ae[:, :, 1:L + 1], convp[:, c, 1:2], ca, ALU.mult, ALU.add), r=[aek, cak, 'convp'], w=[cak])
                P.op('dve', STT(ca, ae[:, :, 2:L + 2], convp[:, c, 2:3], ca, ALU.mult, ALU.add), r=[aek, cak, 'convp'], w=[cak])
                P.op('act', ACT(gl[es_][:, 0:n], cacc[es_][:, 0:n], AF.Gelu_apprx_tanh), r=[cak], w=[('gl', es_)])
                P.op('dve', TT(h_bf[es_][:, 0:n], gl[es_][:, 0:n], u[:, 256:256 + n], ALU.mult), r=[('gl', es_), ukey],
                     w=[('h_bf', es_)])
                if kind == 'main':
                    P.op('pool', CP('pool', halo_all[:, c, :], ae[:, 0, L:L + 2]), r=[aek], w=[('halo', c)])
                    if last_main:
                        P.op('pool', CP('pool', fc_sb[:, c, 0, :], ae[:, 0, L:L + 2]), r=[aek], w=[('fc', c)])
                else:
                    P.op('pool', CP('pool', fc_sb[:, c, 1:5, :], ae[:, :, L:L + 2]), r=[aek], w=[('fc', c)])

            def rec_down(c):
                ci = base + c
                ws = ci % 4
                es_ = ci % 4
                for sub in range(nsub):
                    dd = pD(sub)
                    for half in range(2):
                        P.op('pe', MM(dd[0:ms_, half, :], h_bf[es_][:, sub * 128:sub * 128 + ms_],
                                      wd_sb[ws][:, half * 512:(half + 1) * 512], start=(c == 0), stop=(c == NCH - 1)),
                             r=[('h_bf', es_), ('wd_sb', ws)], w=pDkeys(sub))

            for cc_ in range(NCH + LAG):
                if cc_ < NCH:
                    rec_up(cc_)
                if kind != 'halo' and cc_ >= LAG:
                    rec_down(cc_ - LAG)
            if kind == 'halo':
                return
            orow = row0 - 2
            for sub in range(nsub):
                dd = pD(sub)
                P.op('dve', STT(zs2[0:ms_, sub, :], x1_f[0:ms_, sub, :], ALPHA, dd[0:ms_, :, :].rearrange("p a b -> p (a b)"),
                                ALU.mult, ALU.add), r=[('x1_f', sub)] + pDkeys(sub), w=[('zs2', sub)])
                layer_norm(sub, ms_, zs2, 2, zs2, ('zs2', sub), skey=('zs2', sub))
                P.dma('sp', DMA(y3[orow + sub * 128:orow + sub * 128 + ms_, :], zs2[0:ms_, sub, :]), 'oy', r=[('zs2', sub)])

        def layer_norm(sub, ms_, src, gi, dst, dkey, skey=None):
            skey = skey or ('zs', sub)
            for hf in range(2):
                P.op('dve', lambda e, hf=hf: e.bn_stats(out=bst[0:ms_, sub, hf * 6:(hf + 1) * 6],
                                                        in_=src[0:ms_, sub, hf * 512:(hf + 1) * 512]),
                     r=[skey], w=[('bst', sub)])
            P.op('dve', lambda e: e.bn_aggr(out=mv[0:ms_, sub, 0:2], in_=bst[0:ms_, sub, :]), r=[('bst', sub)], w=[('mv', sub)])
            P.op('act', ACT(mv[0:ms_, sub, 2:3], mv[0:ms_, sub, 1:2], AF.Sqrt, bias=EPS, scale=1.0), r=[('mv', sub)],
                 w=[('mv', sub)])
            P.op('dve', RC(mv[0:ms_, sub, 3:4], mv[0:ms_, sub, 2:3]), r=[('mv', sub)], w=[('mv', sub)])
            P.op('dve', TS(src[0:ms_, sub, :], src[0:ms_, sub, :], mv[0:ms_, sub, 0:1], mv[0:ms_, sub, 3:4],
                           ALU.subtract, ALU.mult), r=[skey, ('mv', sub)], w=[skey])
            P.op('dve', TT(src[0:ms_, sub, :], src[0:ms_, sub, :], lnbc[0:ms_, gi, :], ALU.mult), r=[skey, 'lnbc'], w=[skey])
            P.op('dve', TT(dst[0:ms_, sub, :], src[0:ms_, sub, :], lnbc[0:ms_, gi + 1, :], ALU.add), r=[skey, 'lnbc'],
                 w=[dkey] if dkey != skey else [skey])

        ffn_tile('halo', 0, 2, 1, 2, 0)
        nmain = TQ // 256
        for i in range(nmain):
            ffn_tile('main', 2 + i * 256, 256, 1, 256, i, last_main=(i == nmain - 1))
        ffn_tile('sample', 2 + TQ, 64, 4, 16, 0)
        P.dma('sp', DMA(o_fc[:, :], fc_sb[:, :, :, :].rearrange("p c s t -> p (c s t)")), 'ofc',
              r=[('fc', c) for c in range(NCH)])

        P.emit()
        info = dict(p1_words=p1_total, p3_words=p3_total, n_sems=P.n_sems, n_ops=P.n_ops)
    return nc, info


def host_inputs(inp, TP=8192):
    f = np.float32
    xp = np.asarray(inp["x_prompt"], f)[:, :TP]
    xs = np.asarray(inp["x_sample"], f)
    w_in = np.asarray(inp["w_in"], f)[0]
    ck = np.asarray(inp["cache_diff_k"], f)[0]
    cvv = np.asarray(inp["cache_diff_v"], f)[0]
    sg = np.asarray(inp["state_gla"], f)[0]
    cf = np.asarray(inp["cache_ffn_conv"], f)[0]
    TQ = TP // 4
    ident = np.eye(128, dtype=f)
    s_idx = np.arange(128)
    cm = (s_idx[:, None] <= s_idx[None, :]).astype(f)
    smk = cm * (s_idx[:, None] // 16 == s_idx[None, :] // 16).astype(f)
    rmask = np.ones((128, 2, 256), f)
    rmask[:, 0, 0::128] = 0.0
    rmask[:, 1, 0::16] = 0.0
    bmask = np.zeros((128, 8, 2, 16), f)
    for j in range(8):
        bmask[16 * j:16 * (j + 1), j] = 1.0
    lnbc = np.stack([np.broadcast_to(np.asarray(inp[k], f)[0], (128, D)) for k in ("ln1_g", "ln1_b", "ln2_g", "ln2_b")], axis=1)
    convw = np.asarray(inp["conv_w"], f)[0]
    convb = np.asarray(inp["conv_b"], f)[0]
    convp = np.concatenate([convw, convb[None]], axis=0).reshape(4, NCH, 128).transpose(2, 1, 0)
    lamv = np.broadcast_to(np.stack([np.asarray(inp[k], f)[0] for k in ("lam_q1", "lam_k1", "lam_q2", "lam_k2")], 0), (128, 4, 64))
    gng_bc = np.broadcast_to(np.asarray(inp["gla_norm_g"], f)[0], (128, 256))
    dng = np.asarray(inp["diff_norm_g"], f)[0]
    dng_bc = np.broadcast_to(np.concatenate([dng, dng]), (128, 256))
    offs = [0, 512, 1024, 2048, 3072, 3088, 4112, 5136, 6160, 7184]
    w_out = np.ascontiguousarray(np.asarray(inp["w_out"], f)[0])
    w_up = np.ascontiguousarray(np.asarray(inp["w_up"], f)[0])
    w_down = np.ascontiguousarray(np.asarray(inp["w_down"], f)[0])
    maps = []
    for c in range(NCORES):
        b, g = c // 4, c % 4
        j = g
        sb = slice(16 * b, 16 * b + 16)
        xT1 = np.concatenate([xp[b].T, xs[sb].reshape(256, D).T], axis=1)
        cols = [w_in[:, offs[0] + g * 128:offs[0] + (g + 1) * 128],
                w_in[:, offs[1] + g * 128:offs[1] + (g + 1) * 128],
                w_in[:, offs[5] + g * 256:offs[5] + (g + 1) * 256],
                w_in[:, offs[6] + g * 256:offs[6] + (g + 1) * 256],
                w_in[:, offs[7] + g * 256:offs[7] + (g + 1) * 256],
                w_in[:, offs[2] + g * 256:offs[2] + (g + 1) * 256],
                w_in[:, offs[3] + g * 256:offs[3] + (g + 1) * 256],
                w_in[:, offs[8] + g * 256:offs[8] + (g + 1) * 256],
                w_in[:, offs[9] + g * 256:offs[9] + (g + 1) * 256],
                w_in[:, offs[4]:offs[4] + 16]]
        w_in1 = np.concatenate(cols, axis=1)
        ckT = ck[sb][:, :, 2 * g:2 * g + 2].reshape(16, 1024, 2, 128).transpose(0, 2, 3, 1)
        cvh = cvv[sb][:, :, 2 * g:2 * g + 2].reshape(16, 8, 128, 2, 128).transpose(0, 3, 2, 1, 4)
        x3 = np.zeros((2 + TQ + 64, D), f)
        if j > 0:
            x3[0:2] = xp[b, TQ * j - 2:TQ * j]
        x3[2:2 + TQ] = xp[b, TQ * j:TQ * (j + 1)]
        x3[2 + TQ:] = xs[4 * c:4 * c + 4].reshape(64, D)
        cfc = cf[4 * c:4 * c + 4].reshape(4, 2, NCH, 128).transpose(3, 2, 0, 1)
        m = dict(
            xT1=xT1, w_in1=w_in1, w_a2g=np.asarray(inp["w_a2"], f)[0][:, g * 128:(g + 1) * 128],
            b_ag=np.asarray(inp["b_a"], f)[0][g * 128:(g + 1) * 128].reshape(128, 1),
            gng_bc=gng_bc, dng_bc=dng_bc, lamv=lamv, ckT=ckT, cv=cvh, s0=sg[sb, g], ident=ident, cm=cm, smk=smk,
            rmask=rmask, bmask=bmask, x3=x3, w_out=w_out, w_up=w_up, w_down=w_down, lnbc=lnbc, convp=convp, cfc=cfc,
            flag=np.full((128, 1), 1.0 if j > 0 else 0.0, f))
        maps.append({k: np.ascontiguousarray(v, dtype=f) for k, v in m.items()})
    return maps


def assemble(res, TP=8192):
    f = np.float32
    TQ = TP // 4
    yp = np.zeros((2, TP, D), f)
    ys = np.zeros((32, 16, D), f)
    pk = np.zeros((1, 2, TP, 8, 2, 64), f)
    pv = np.zeros((1, 2, TP, 8, 128), f)
    pS = np.zeros((1, 2, 4, 128, 256), f)
    pc = np.zeros((1, 2, 2, DFF), f)
    sk = np.zeros((1, 32, 16, 8, 2, 64), f)
    sv = np.zeros((1, 32, 16, 8, 128), f)
    sS = np.zeros((1, 32, 4, 128, 256), f)
    sc = np.zeros((1, 32, 2, DFF), f)
    for c in range(NCORES):
        r = res[c]
        b, g = c // 4, c % 4
        j = g
        okv = r["o_kv"]
        pk[0, b, :, 2 * g:2 * g + 2] = okv[:TP, 0:256].reshape(TP, 2, 2, 64)
        pv[0, b, :, 2 * g:2 * g + 2] = okv[:TP, 256:512].reshape(TP, 2, 128)
        sk[0, 16 * b:16 * b + 16, :, 2 * g:2 * g + 2] = okv[TP:, 0:256].reshape(16, 16, 2, 2, 64)
        sv[0, 16 * b:16 * b + 16, :, 2 * g:2 * g + 2] = okv[TP:, 256:512].reshape(16, 16, 2, 128)
        pS[0, b, g] = r["o_S"][0]
        sS[0, 16 * b:16 * b + 16, g] = r["o_S"][1:]
        yp[b, TQ * j:TQ * (j + 1)] = r["y3"][:TQ]
        ys[4 * c:4 * c + 4] = r["y3"][TQ:].reshape(4, 16, D)
        fc = r["o_fc"].reshape(128, NCH, 5, 2).transpose(2, 3, 1, 0).reshape(5, 2, DFF)
        if j == 3:
            pc[0, b] = fc[0]
        sc[0, 4 * c:4 * c + 4] = fc[1:]
    return (yp, ys, pk, pv, pS, pc, sk, sv, sS, sc)


_CACHE = {}


def kernel(**inputs):
    TP = 8192
    if TP not in _CACHE:
        _CACHE[TP] = build(TP)[0]
    nc = _CACHE[TP]
    maps = host_inputs(inputs, TP)
    out = run_bass_kernel_spmd(nc, maps, core_ids=list(range(NCORES)))
    return assemble(out.results, TP)
```
